# Optimizing a Trainium2 kernel written in Bass

```python
import math
import jax
import jax.numpy as jnp
from jax import lax
import numpy as np


D_MODEL = 1024
BATCH = 2
SEQ = 16384
DEPTH = 4

GRID_W = 64
CTX_LEN = 256
N_MIXERS = 3
D_FF = 4 * D_MODEL
NORM_EPS = 1e-6

DN_K_HEADS = 8
DN_V_HEADS = 16
DN_HEAD_K = 128
DN_HEAD_V = 128
DN_GROUP = DN_V_HEADS // DN_K_HEADS
DN_K_DIM = DN_K_HEADS * DN_HEAD_K
DN_V_DIM = DN_V_HEADS * DN_HEAD_V
DN_QKV_DIM = 2 * DN_K_DIM + DN_V_DIM
DN_IN_DIM = DN_QKV_DIM + DN_V_DIM + 4 * DN_V_HEADS
SHORT_CONV = 5
DN_CHUNK = 64

GLA_HEADS = 4
GLA_K_DIM = D_MODEL // 2
GLA_V_DIM = D_MODEL
GLA_HEAD_K = GLA_K_DIM // GLA_HEADS
GLA_HEAD_V = GLA_V_DIM // GLA_HEADS
GLA_GATE_RANK = 16
GLA_GATE_NORMALIZER = 16.0
GLA_IN_DIM = 2 * GLA_K_DIM + 2 * GLA_V_DIM + 2 * GLA_GATE_RANK
GLA_CHUNK = 64

ATTN_Q_HEADS = 8
ATTN_KV_HEADS = 2
ATTN_HEAD_DIM = 128
ATTN_GROUP = ATTN_Q_HEADS // ATTN_KV_HEADS
ATTN_Q_DIM = ATTN_Q_HEADS * ATTN_HEAD_DIM
ATTN_KV_DIM = ATTN_KV_HEADS * ATTN_HEAD_DIM
ATTN_IN_DIM = ATTN_Q_DIM + 2 * ATTN_KV_DIM
Q_BLOCK = 128
ROPE_THETA = 10000.0

kernel_name = 'hybrid_dit_deltanet_gla_gqa'

F32 = jnp.float32


def rms_norm(x, g):
    xf = x.astype(F32)
    y = xf * lax.rsqrt(jnp.mean(xf * xf, axis=-1, keepdims=True) + NORM_EPS)
    return (y * g.astype(F32)).astype(x.dtype)


def l2_normalize(x):
    xf = x.astype(F32)
    return xf * lax.rsqrt(jnp.sum(xf * xf, axis=-1, keepdims=True) + NORM_EPS)


def modulate(h, shift, scale):
    return h * (1.0 + scale) + shift


def squared_relu_ffn(h, w1, w2):
    return jnp.square(jax.nn.relu(h @ w1)) @ w2


def short_conv(u, w):
    n = u.shape[1]
    pad = SHORT_CONV // 2
    up = jnp.pad(u, ((0, 0), (pad, pad), (0, 0)))
    out = up[:, 0:n] * w[:, 0]
    for tap in range(1, SHORT_CONV):
        out = out + up[:, tap:tap + n] * w[:, tap]
    return out


def to_chunks(t, chunk):
    b, h, n = t.shape[:3]
    t = t.reshape((b, h, n // chunk, chunk) + t.shape[3:])
    return jnp.moveaxis(t, 2, 0)


def from_chunks(t):
    t = jnp.moveaxis(t, 0, 2)
    b, h, nc, chunk = t.shape[:4]
    return t.reshape((b, h, nc * chunk) + t.shape[4:])


def time_flip(t):
    return jnp.flip(t, axis=2)


def identity(t):
    return t


def axial_rope_tables(n_rows):
    row = jnp.repeat(jnp.arange(n_rows), GRID_W).astype(F32)
    col = jnp.tile(jnp.arange(GRID_W), n_rows).astype(F32)
    axis_dim = ATTN_HEAD_DIM // 2
    inv_freq = jnp.power(ROPE_THETA, -jnp.arange(0, axis_dim, 2, dtype=F32) / axis_dim)
    ang_row = row[:, None] * inv_freq
    ang_col = col[:, None] * inv_freq
    return jnp.cos(ang_row), jnp.sin(ang_row), jnp.cos(ang_col), jnp.sin(ang_col)


def rotate_pairs(x, cos, sin):
    x1, x2 = jnp.split(x, 2, axis=-1)
    cs, sn = cos[:, None, :], sin[:, None, :]
    return jnp.concatenate([x1 * cs - x2 * sn, x1 * sn + x2 * cs], axis=-1)


def apply_axial_rope(x, tables):
    cos_r, sin_r, cos_c, sin_c = tables
    xf = x.astype(F32)
    half = ATTN_HEAD_DIM // 2
    y = jnp.concatenate([rotate_pairs(xf[..., :half], cos_r, sin_r),
                         rotate_pairs(xf[..., half:], cos_c, sin_c)], axis=-1)
    return y.astype(x.dtype)


def gated_delta_rule(q, k, v, g, beta, state):
    c = DN_CHUNK
    tri = jnp.tril(jnp.ones((c, c), dtype=bool))
    strict = jnp.tril(jnp.ones((c, c), dtype=bool), -1)
    eye = jnp.eye(c, dtype=F32)

    def step(s, xs):
        qc, kc, vc, gc, bc = xs
        qc = jnp.repeat(qc, DN_GROUP, axis=1)
        kc = jnp.repeat(kc, DN_GROUP, axis=1)
        gcum = jnp.cumsum(gc, axis=-1)
        decay = jnp.exp(jnp.where(tri, gcum[..., :, None] - gcum[..., None, :], -jnp.inf))
        kk = jnp.einsum('bhid,bhjd->bhij', kc, kc)
        m = jnp.where(strict, bc[..., :, None] * kk * decay, 0.0)
        rhs = jnp.concatenate([vc * bc[..., None], kc * (bc * jnp.exp(gcum))[..., None]], axis=-1)
        sol = lax.linalg.triangular_solve(eye + m, rhs, left_side=True, lower=True, unit_diagonal=True)
        u, w = sol[..., :DN_HEAD_V], sol[..., DN_HEAD_V:]
        v_new = u - jnp.einsum('bhid,bhdv->bhiv', w, s)
        a_qk = jnp.einsum('bhid,bhjd->bhij', qc, kc) * decay
        o = (jnp.einsum('bhid,bhdv->bhiv', qc * jnp.exp(gcum)[..., None], s)
             + jnp.einsum('bhij,bhjv->bhiv', a_qk, v_new))
        tail = jnp.exp(gcum[..., -1:] - gcum)
        s = (s * jnp.exp(gcum[..., -1])[..., None, None]
             + jnp.einsum('bhjd,bhjv->bhdv', kc * tail[..., None], v_new))
        return s, o

    xs = (to_chunks(q, c), to_chunks(k, c), to_chunks(v, c), to_chunks(g, c), to_chunks(beta, c))
    state, o = lax.scan(step, state, xs)
    return from_chunks(o), state


def gla_rule(q, k, v, gk, state):
    c = GLA_CHUNK
    tri = jnp.tril(jnp.ones((c, c), dtype=bool))[:, :, None]

    def step(s, xs):
        qc, kc, vc, gc = xs
        bcum = jnp.cumsum(gc, axis=-2)
        rel = jnp.exp(jnp.where(tri, bcum[..., :, None, :] - bcum[..., None, :, :], -jnp.inf))
        a_qk = jnp.einsum('bhid,bhjd,bhijd->bhij', qc, kc, rel)
        o = (jnp.einsum('bhij,bhjv->bhiv', a_qk, vc)
             + jnp.einsum('bhid,bhdv->bhiv', qc * jnp.exp(bcum), s))
        last = bcum[..., -1:, :]
        s = (s * jnp.exp(last[..., 0, :])[..., None]
             + jnp.einsum('bhjd,bhjv->bhdv', kc * jnp.exp(last - bcum), vc))
        return s, o

    xs = (to_chunks(q, c), to_chunks(k, c), to_chunks(v, c), to_chunks(gk, c))
    state, o = lax.scan(step, state, xs)
    return from_chunks(o), state


def deltanet_mixer(hx, hc, w_in, conv_w, a_log, dt_bias, norm_g, w_out, with_ctx_out):
    def project(h):
        bsz, n, _ = h.shape
        p = h @ w_in
        qkv = jax.nn.silu(short_conv(p[..., :DN_QKV_DIM], conv_w))
        z = p[..., DN_QKV_DIM:DN_QKV_DIM + DN_V_DIM]
        ab = p[..., DN_QKV_DIM + DN_V_DIM:].astype(F32)
        q = l2_normalize(qkv[..., :DN_K_DIM].reshape(bsz, n, DN_K_HEADS, DN_HEAD_K)) * DN_HEAD_K ** -0.5
        k = l2_normalize(qkv[..., DN_K_DIM:2 * DN_K_DIM].reshape(bsz, n, DN_K_HEADS, DN_HEAD_K))
        v = qkv[..., 2 * DN_K_DIM:].reshape(bsz, n, DN_V_HEADS, DN_HEAD_V).astype(F32)
        a = ab[..., :2 * DN_V_HEADS].reshape(bsz, n, 2, DN_V_HEADS)
        b = ab[..., 2 * DN_V_HEADS:].reshape(bsz, n, 2, DN_V_HEADS)
        g = -jnp.exp(a_log.astype(F32)) * jax.nn.softplus(a + dt_bias.astype(F32))
        beta = jax.nn.sigmoid(b)
        return (jnp.swapaxes(q, 1, 2), jnp.swapaxes(k, 1, 2), jnp.swapaxes(v, 1, 2),
                jnp.moveaxis(g, 1, -1), jnp.moveaxis(beta, 1, -1), z)

    qx, kx, vx, gx, bx, zx = project(hx)
    qc, kc, vc, gc, bc, zc = project(hc)
    bsz = hx.shape[0]
    out_x, out_c = [], []
    for d in range(2):
        flip = time_flip if d == 1 else identity
        state0 = jnp.zeros((bsz, DN_V_HEADS, DN_HEAD_K, DN_HEAD_V), F32)
        oc, state_c = gated_delta_rule(flip(qc), flip(kc), flip(vc), flip(gc[:, d]), flip(bc[:, d]), state0)
        ox, _ = gated_delta_rule(flip(qx), flip(kx), flip(vx), flip(gx[:, d]), flip(bx[:, d]), state_c)
        out_x.append(flip(ox))
        out_c.append(flip(oc))

    def readout(o, z):
        bsz_, n = z.shape[:2]
        o = rms_norm(jnp.swapaxes(o, 1, 2), norm_g)
        o = o * jax.nn.silu(z.reshape(bsz_, n, DN_V_HEADS, DN_HEAD_V).astype(F32))
        return o.reshape(bsz_, n, DN_V_DIM).astype(z.dtype) @ w_out

    yx = readout(out_x[0] + out_x[1], zx)
    yc = readout(out_c[0] + out_c[1], zc) if with_ctx_out else None
    return yx, yc


def gla_mixer(hx, hc, w_in, gate_w2, gate_b2, norm_g, w_out, with_ctx_out):
    def project(h):
        bsz, n, _ = h.shape
        p = h @ w_in
        q = p[..., :GLA_K_DIM].reshape(bsz, n, GLA_HEADS, GLA_HEAD_K).astype(F32) * GLA_HEAD_K ** -0.5
        k = p[..., GLA_K_DIM:2 * GLA_K_DIM].reshape(bsz, n, GLA_HEADS, GLA_HEAD_K).astype(F32)
        v = p[..., 2 * GLA_K_DIM:2 * GLA_K_DIM + GLA_V_DIM].reshape(bsz, n, GLA_HEADS, GLA_HEAD_V).astype(F32)
        gate = p[..., 2 * GLA_K_DIM + GLA_V_DIM:2 * GLA_K_DIM + 2 * GLA_V_DIM]
        low = p[..., 2 * GLA_K_DIM + 2 * GLA_V_DIM:].reshape(bsz, n, 2, GLA_GATE_RANK)
        logits = jnp.einsum('btdr,drk->btdk', low, gate_w2) + gate_b2
        gk = jax.nn.log_sigmoid(logits.astype(F32)) / GLA_GATE_NORMALIZER
        gk = gk.reshape(bsz, n, 2, GLA_HEADS, GLA_HEAD_K).transpose(0, 2, 3, 1, 4)
        return jnp.swapaxes(q, 1, 2), jnp.swapaxes(k, 1, 2), jnp.swapaxes(v, 1, 2), gk, gate

    qx, kx, vx, gx, zx = project(hx)
    qc, kc, vc, gc, zc = project(hc)
    bsz = hx.shape[0]
    out_x, out_c = [], []
    for d in range(2):
        flip = time_flip if d == 1 else identity
        state0 = jnp.zeros((bsz, GLA_HEADS, GLA_HEAD_K, GLA_HEAD_V), F32)
        oc, state_c = gla_rule(flip(qc), flip(kc), flip(vc), flip(gc[:, d]), state0)
        ox, _ = gla_rule(flip(qx), flip(kx), flip(vx), flip(gx[:, d]), state_c)
        out_x.append(flip(ox))
        out_c.append(flip(oc))

    def readout(o, z):
        bsz_, n = z.shape[:2]
        o = rms_norm(jnp.swapaxes(o, 1, 2), norm_g)
        o = o * jax.nn.silu(z.reshape(bsz_, n, GLA_HEADS, GLA_HEAD_V).astype(F32))
        return o.reshape(bsz_, n, GLA_V_DIM).astype(z.dtype) @ w_out

    yx = readout(out_x[0] + out_x[1], zx)
    yc = readout(out_c[0] + out_c[1], zc) if with_ctx_out else None
    return yx, yc


def attention_mixer(hx, hc, w_in, q_norm_g, k_norm_g, w_out, rope, with_ctx_out):
    def project(h):
        bsz, n, _ = h.shape
        p = h @ w_in
        q = p[..., :ATTN_Q_DIM].reshape(bsz, n, ATTN_Q_HEADS, ATTN_HEAD_DIM)
        k = p[..., ATTN_Q_DIM:ATTN_Q_DIM + ATTN_KV_DIM].reshape(bsz, n, ATTN_KV_HEADS, ATTN_HEAD_DIM)
        v = p[..., ATTN_Q_DIM + ATTN_KV_DIM:].reshape(bsz, n, ATTN_KV_HEADS, ATTN_HEAD_DIM)
        return rms_norm(q, q_norm_g), rms_norm(k, k_norm_g), v

    qx, kx, vx = project(hx)
    qc, kc, vc = project(hc)
    qx = apply_axial_rope(qx, rope)
    kx = apply_axial_rope(kx, rope)
    bsz, n = hx.shape[:2]
    scale = ATTN_HEAD_DIM ** -0.5

    def attend(q_blk, keys, values):
        s = jnp.einsum('bkgqd,bksd->bkgqs', q_blk, keys).astype(F32) * scale
        p = jax.nn.softmax(s, axis=-1).astype(values.dtype)
        return jnp.einsum('bkgqs,bksd->bkgqd', p, values)

    k_all = jnp.concatenate([kx, kc], axis=1).transpose(0, 2, 1, 3)
    v_all = jnp.concatenate([vx, vc], axis=1).transpose(0, 2, 1, 3)
    n_blk = n // Q_BLOCK
    q_blocks = qx.reshape(bsz, n_blk, Q_BLOCK, ATTN_KV_HEADS, ATTN_GROUP, ATTN_HEAD_DIM).transpose(1, 0, 3, 4, 2, 5)
    o = lax.map(lambda qb: attend(qb, k_all, v_all), q_blocks)
    yx = o.transpose(1, 0, 4, 2, 3, 5).reshape(bsz, n, ATTN_Q_DIM) @ w_out
    yc = None
    if with_ctx_out:
        lc = hc.shape[1]
        q_ctx = qc.reshape(bsz, lc, ATTN_KV_HEADS, ATTN_GROUP, ATTN_HEAD_DIM).transpose(0, 2, 3, 1, 4)
        oc = attend(q_ctx, kc.transpose(0, 2, 1, 3), vc.transpose(0, 2, 1, 3))
        yc = oc.transpose(0, 3, 1, 2, 4).reshape(bsz, lc, ATTN_Q_DIM) @ w_out
    return yx, yc


def setup_inputs(seed: int = 0) -> dict:
    key = jax.random.key(seed)
    ks = jax.random.split(key, 32)
    n_a = len(range(0, DEPTH, N_MIXERS))
    n_b = len(range(1, DEPTH, N_MIXERS))
    n_c = len(range(2, DEPTH, N_MIXERS))

    def normal(k, shape, scale):
        return jax.random.normal(k, shape, F32) * scale

    def gain(k, shape):
        return 1.0 + normal(k, shape, 0.02)

    dt = jnp.exp(jax.random.uniform(ks[12], (n_a, 2, DN_V_HEADS), F32, math.log(1e-3), math.log(1e-1)))
    return {
        'x': normal(ks[0], (BATCH, SEQ, D_MODEL), 1.0),
        'c': normal(ks[1], (BATCH, D_MODEL), 1.0),
        'ctx': normal(ks[2], (BATCH, CTX_LEN, D_MODEL), 1.0),
        'c_ctx': normal(ks[3], (D_MODEL,), 1.0),
        'ada_w': normal(ks[4], (DEPTH, D_MODEL, 6 * D_MODEL), 0.5 * D_MODEL ** -0.5),
        'ada_b': normal(ks[5], (DEPTH, 6 * D_MODEL), 0.01),
        'norm_mix_g': gain(ks[6], (DEPTH, D_MODEL)),
        'norm_ffn_g': gain(ks[7], (DEPTH, D_MODEL)),
        'ffn_w1': normal(ks[8], (DEPTH, D_MODEL, D_FF), D_MODEL ** -0.5),
        'ffn_w2': normal(ks[9], (DEPTH, D_FF, D_MODEL), D_FF ** -0.5),
        'dn_w_in': normal(ks[10], (n_a, D_MODEL, DN_IN_DIM), D_MODEL ** -0.5),
        'dn_conv_w': normal(ks[11], (n_a, DN_QKV_DIM, SHORT_CONV), SHORT_CONV ** -0.5),
        'dn_a_log': jnp.log(jax.random.uniform(ks[13], (n_a, 2, DN_V_HEADS), F32, 1.0, 16.0)),
        'dn_dt_bias': dt + jnp.log(-jnp.expm1(-dt)),
        'dn_norm_g': gain(ks[14], (n_a, DN_HEAD_V)),
        'dn_w_out': normal(ks[15], (n_a, DN_V_DIM, D_MODEL), DN_V_DIM ** -0.5),
        'gla_w_in': normal(ks[16], (n_b, D_MODEL, GLA_IN_DIM), D_MODEL ** -0.5),
        'gla_gate_w2': normal(ks[17], (n_b, 2, GLA_GATE_RANK, GLA_K_DIM), GLA_GATE_RANK ** -0.5),
        'gla_gate_b2': normal(ks[18], (n_b, 2, GLA_K_DIM), 0.1),
        'gla_norm_g': gain(ks[19], (n_b, GLA_HEAD_V)),
        'gla_w_out': normal(ks[20], (n_b, GLA_V_DIM, D_MODEL), GLA_V_DIM ** -0.5),
        'attn_w_in': normal(ks[21], (n_c, D_MODEL, ATTN_IN_DIM), D_MODEL ** -0.5),
        'attn_q_norm_g': gain(ks[22], (n_c, ATTN_HEAD_DIM)),
        'attn_k_norm_g': gain(ks[23], (n_c, ATTN_HEAD_DIM)),
        'attn_w_out': normal(ks[24], (n_c, ATTN_Q_DIM, D_MODEL), ATTN_Q_DIM ** -0.5),
    }


def reference(x, c, ctx, c_ctx, ada_w, ada_b, norm_mix_g, norm_ffn_g, ffn_w1, ffn_w2,
              dn_w_in, dn_conv_w, dn_a_log, dn_dt_bias, dn_norm_g, dn_w_out,
              gla_w_in, gla_gate_w2, gla_gate_b2, gla_norm_g, gla_w_out,
              attn_w_in, attn_q_norm_g, attn_k_norm_g, attn_w_out):
    n_tok = x.shape[1]
    ROWS = n_tok // GRID_W
    rope = axial_rope_tables(ROWS)
    for i in range(DEPTH):
        with_ctx_out = i < DEPTH - 1
        mix, slot = i % N_MIXERS, i // N_MIXERS
        mod_x = jnp.split((jax.nn.silu(c) @ ada_w[i] + ada_b[i])[:, None, :], 6, axis=-1)
        mod_c = jnp.split(jax.nn.silu(c_ctx) @ ada_w[i] + ada_b[i], 6, axis=-1)
        hx = modulate(rms_norm(x, norm_mix_g[i]), mod_x[0], mod_x[1])
        hc = modulate(rms_norm(ctx, norm_mix_g[i]), mod_c[0], mod_c[1])
        if mix == 0:
            yx, yc = deltanet_mixer(hx, hc, dn_w_in[slot], dn_conv_w[slot], dn_a_log[slot], dn_dt_bias[slot],
                                    dn_norm_g[slot], dn_w_out[slot], with_ctx_out)
        elif mix == 1:
            yx, yc = gla_mixer(hx, hc, gla_w_in[slot], gla_gate_w2[slot], gla_gate_b2[slot],
                               gla_norm_g[slot], gla_w_out[slot], with_ctx_out)
        else:
            yx, yc = attention_mixer(hx, hc, attn_w_in[slot], attn_q_norm_g[slot], attn_k_norm_g[slot],
                                     attn_w_out[slot], rope, with_ctx_out)
        x = x + mod_x[2] * yx
        hx = modulate(rms_norm(x, norm_ffn_g[i]), mod_x[3], mod_x[4])
        x = x + mod_x[5] * squared_relu_ffn(hx, ffn_w1[i], ffn_w2[i])
        if with_ctx_out:
            ctx = ctx + mod_c[2] * yc
            hc = modulate(rms_norm(ctx, norm_ffn_g[i]), mod_c[3], mod_c[4])
            ctx = ctx + mod_c[5] * squared_relu_ffn(hc, ffn_w1[i], ffn_w2[i])
    return x
```

```python
import numpy as np
import concourse.bass as bass
import concourse.mybir as mybir
from concourse.bass_utils import run_bass_kernel_spmd

F32 = mybir.dt.float32
BF16 = mybir.dt.bfloat16
ALU = mybir.AluOpType
AF = mybir.ActivationFunctionType
AX = mybir.AxisListType

NCORES = 8


class Res:
    __slots__ = ("name", "w", "r")

    def __init__(self, name):
        self.name = name
        self.w = None
        self.r = []


class Sched:
    ENGS = ("pe", "act", "dve", "pool", "sp")

    def __init__(self, nc, stack):
        self.nc = nc
        self.stack = stack
        self.ops = {e: [] for e in self.ENGS}
        self.sem = {e: stack.enter_context(nc.semaphore("s_" + e)) for e in self.ENGS}
        self.cnt = {e: 0 for e in self.ENGS}
        self.known = {e: {} for e in self.ENGS}
        self.slots = {}
        self.nres = 0

    def res(self, name=None):
        self.nres += 1
        return Res(name or ("r%d" % self.nres))

    def sb(self, name, shape, dt):
        t = self.stack.enter_context(self.nc.sbuf_tensor(name, list(shape), dt))
        return t

    def ps(self, name, shape, dt=F32):
        t = self.stack.enter_context(self.nc.psum_tensor(name, list(shape), dt))
        return t

    def _slot(self, name):
        if name not in self.slots:
            self.slots[name] = [self.stack.enter_context(self.nc.semaphore("d_" + name)), 0]
        return self.slots[name]

    def _deps(self, eng, reads, writes):
        need = {}

        def add(h):
            if h is None:
                return
            s, v, e = h
            if e == eng and eng == "pe":
                return
            if need.get(s, (0,))[0] < v:
                need[s] = (v, s)

        for r in reads:
            add(r.w)
        for w in writes:
            add(w.w)
            for h in w.r:
                add(h)
        waits = []
        kn = self.known[eng]
        for s, (v, _) in need.items():
            if kn.get(id(s), 0) >= v:
                continue
            kn[id(s)] = v
            waits.append((s, v))
        return waits

    def op(self, eng, fn, reads=(), writes=()):
        waits = self._deps(eng, reads, writes)
        self.cnt[eng] += 1
        h = (self.sem[eng], self.cnt[eng], eng)
        self.ops[eng].append((waits, fn, self.sem[eng], 1))
        for r in reads:
            r.r.append(h)
        for w in writes:
            w.w = h
            w.r = []
        return h

    def dma(self, eng, slot, out, in_, reads=(), writes=(), **kw):
        waits = self._deps(eng, reads, writes)
        sl = self._slot(slot)
        sl[1] += 16
        h = (sl[0], sl[1], "dma")
        self.ops[eng].append((waits, lambda e: e.dma_start(out=out, in_=in_, **kw), sl[0], 16))
        for r in reads:
            r.r.append(h)
        for w in writes:
            w.w = h
            w.r = []
        return h

    def barrier(self):
        targets = [(self.sem[e], self.cnt[e]) for e in self.ENGS if self.cnt[e] > 0]
        targets += [(sl[0], sl[1]) for sl in self.slots.values() if sl[1] > 0]
        for e in self.ENGS:
            waits = []
            for s_, v in targets:
                if s_ is self.sem[e]:
                    continue
                if self.known[e].get(id(s_), 0) < v:
                    self.known[e][id(s_)] = v
                    waits.append((s_, v))
            self.ops[e].append((waits, None, None, 0))

    def final_wait(self, eng, ress):
        waits = self._deps(eng, ress, ())
        self.ops[eng].append((waits, None, None, 0))

    def emit(self):
        nc = self.nc
        with nc.Block() as block:
            def run(engname):
                def body(e):
                    for waits, fn, sem, inc in self.ops[engname]:
                        for s, v in waits:
                            e.wait_ge(s, v)
                        if fn is not None:
                            fn(e).then_inc(sem, inc)
                return body

            block.tensor(run("pe"))
            block.scalar(run("act"))
            block.vector(run("dve"))
            block.gpsimd(run("pool"))
            block.sync(run("sp"))


def build_mod():
    from contextlib import ExitStack
    nc = bass.Bass("TRN2", target_bir_lowering=False)
    ccT = nc.dram_tensor("ccT", [128, 8, 3], F32, kind="ExternalInput").ap()
    w = nc.dram_tensor("w", [4, 1024, 768], F32, kind="ExternalInput").ap()
    b = nc.dram_tensor("b", [4, 3, 768], F32, kind="ExternalInput").ap()
    out = nc.dram_tensor("out", [4, 3, 768], F32, kind="ExternalOutput").ap()
    with ExitStack() as st:
        S = Sched(nc, st)
        cc = S.sb("cc", [128, 8, 3], F32); r_cc = S.res()
        sc = S.sb("sc", [128, 8, 3], F32); r_sc = S.res()
        wt = [S.sb("wt%d" % i, [128, 8, 768], F32) for i in range(2)]
        r_wt = [S.res() for _ in range(2)]
        bt = S.sb("bt", [3, 4, 768], F32); r_bt = S.res()
        ot = S.sb("ot", [3, 4, 768], F32); r_ot = S.res()
        pst = [S.ps("ps%d" % i, [3, 384]) for i in range(2)]
        r_ps = [S.res() for _ in range(2)]
        S.dma("sp", "cc", cc[:], ccT, writes=[r_cc])
        S.dma("sp", "bt", bt[:], b.rearrange("l r n -> r l n"), writes=[r_bt])
        S.op("act", lambda e: e.activation(out=sc[:], in_=cc[:], func=AF.Silu), reads=[r_cc], writes=[r_sc])
        for l in range(4):
            S.dma("sp", "w%d" % (l % 2), wt[l % 2][:], w[l].rearrange("(kc p) n -> p kc n", p=128), writes=[r_wt[l % 2]])
            for hf in range(2):
                for kc in range(8):
                    S.op("pe", lambda e, kc=kc, hf=hf, l=l: e.matmul(pst[hf][:], lhsT=sc[:, kc, :], rhs=wt[l % 2][:, kc, hf * 384:(hf + 1) * 384], start=(kc == 0), stop=(kc == 7)),
                         reads=[r_sc, r_wt[l % 2]], writes=[r_ps[hf]])
                S.op("dve", lambda e, hf=hf, l=l: e.tensor_tensor(out=ot[:, l, hf * 384:(hf + 1) * 384], in0=pst[hf][:], in1=bt[:, l, hf * 384:(hf + 1) * 384], op=ALU.add),
                     reads=[r_ps[hf], r_bt], writes=[r_ot])
        S.dma("sp", "out", out.rearrange("l r n -> r l n"), ot[:], reads=[r_ot], writes=[S.res()])
        S.final_wait("sp", [])
        S.ops["sp"].append(([(S.slots["out"][0], S.slots["out"][1])], None, None, 0))
        S.emit()
    return nc


def run_mod(c, c_ctx, ada_w, ada_b):
    cc = np.concatenate([c, c_ctx[None]], 0).astype(np.float32)
    ccT = np.ascontiguousarray(cc.T.reshape(8, 128, 3).transpose(1, 0, 2))
    nc = build_mod()
    in_maps = []
    for i in range(NCORES):
        sl = slice(i * 768, (i + 1) * 768)
        in_maps.append({"ccT": ccT, "w": np.ascontiguousarray(ada_w[:, :, sl]),
                        "b": np.ascontiguousarray(np.broadcast_to(ada_b[:, None, sl], (4, 3, 768)))})
    res = run_bass_kernel_spmd(nc, in_maps, core_ids=list(range(NCORES)))
    return np.concatenate([r["out"] for r in res.results], axis=2)


from contextlib import ExitStack

EPS = 1e-6
RX = 4096
RT = RX + 128


def _load_weight_bf16(S, w_ap, dst, nk, ncols, stg, r_stg, r_dst, eng_cycle, name):
    CH = 2048
    idx = 0
    for k in range(nk):
        for c0 in range(0, ncols, CH):
            cw = min(CH, ncols - c0)
            b = idx % 2
            S.dma("sp", "%s_stg%d" % (name, b), stg[b][:, :cw], w_ap[k * 128:(k + 1) * 128, c0:c0 + cw], writes=[r_stg[b]])
            eng = eng_cycle[idx % len(eng_cycle)]
            if eng == "act":
                S.op("act", lambda e, b=b, k=k, c0=c0, cw=cw: e.copy(out=dst[:, k, c0:c0 + cw], in_=stg[b][:, :cw]), reads=[r_stg[b]], writes=[r_dst])
            else:
                S.op(eng, lambda e, b=b, k=k, c0=c0, cw=cw: e.tensor_copy(out=dst[:, k, c0:c0 + cw], in_=stg[b][:, :cw]), reads=[r_stg[b]], writes=[r_dst])
            idx += 1


class Front:
    def __init__(self, S, ident_bf, r_ident):
        self.S = S
        self.ident = ident_bf
        self.r_ident = r_ident
        self.junk = S.sb("fr_junk", [128, 1024], F32); self.r_junk = S.res()
        self.ss = S.sb("fr_ss", [128, 4], F32); self.r_ss = S.res()
        self.rstd = S.sb("fr_rstd", [128, 4], F32); self.r_rstd = S.res()
        self.xn = [S.sb("fr_xn%d" % i, [128, 1024], BF16) for i in range(2)]
        self.r_xn = [S.res() for _ in range(2)]
        self.pT = [S.ps("fr_pT%d" % i, [128, 8, 128], BF16) for i in range(1)]
        self.r_pT = [S.res() for _ in range(1)]
        self.n = 0

    def run(self, x_ap, r_x, gs_ap, sh_ap, r_vec, hT_ap, r_hT):
        S = self.S
        i = self.n % 2
        self.n += 1
        xn, r_xn = self.xn[i], self.r_xn[i]
        pT, r_pT = self.pT[0], self.r_pT[0]
        c = self.n % 4
        ss, rstd = self.ss[:, c:c + 1], self.rstd[:, c:c + 1]
        S.op("act", lambda e: e.activation(out=self.junk[:], in_=x_ap, func=AF.Square, accum_out=ss), reads=[r_x], writes=[self.r_junk, self.r_ss])
        S.op("dve", lambda e: e.tensor_scalar(out=rstd, in0=ss, scalar1=1.0 / 1024, scalar2=EPS, op0=ALU.mult, op1=ALU.add), reads=[self.r_ss], writes=[self.r_rstd])
        S.op("act", lambda e: e.activation(out=rstd, in_=rstd, func=AF.Sqrt), reads=[self.r_rstd], writes=[self.r_rstd])
        S.op("dve", lambda e: e.reciprocal(out=rstd, in_=rstd), reads=[self.r_rstd], writes=[self.r_rstd])
        S.op("act", lambda e: e.activation(out=xn[:], in_=x_ap, func=AF.Copy, scale=rstd), reads=[r_x, self.r_rstd], writes=[r_xn])
        for kc in range(8):
            S.op("pe", lambda e, kc=kc: e.transpose(out=pT[:, kc, :], in_=xn[:, kc * 128:(kc + 1) * 128], identity=self.ident[:]), reads=[r_xn, self.r_ident], writes=[r_pT])
        S.op("dve", lambda e: e.tensor_tensor(out=hT_ap, in0=pT[:], in1=gs_ap.unsqueeze(2).to_broadcast([128, 8, 128]), op=ALU.mult), reads=[r_pT, r_vec], writes=[r_hT])
        S.op("pool", lambda e: e.tensor_tensor(out=hT_ap, in0=hT_ap, in1=sh_ap.unsqueeze(2).to_broadcast([128, 8, 128]), op=ALU.add), reads=[r_hT, r_vec], writes=[r_hT])


def _make_ident(S, dt, name):
    t = S.sb(name, [128, 128], dt)
    r = S.res()
    S.op("pool", lambda e: e.memset(t[:], 1.0), writes=[r])
    S.op("pool", lambda e: e.affine_select(out=t[:], in_=t[:], pattern=[[-1, 128]], compare_op=ALU.is_equal, fill=0.0, base=0, channel_multiplier=1), reads=[r], writes=[r])
    return t, r


def _gs_prologue(S, vecF, r_vecF, gi, pairs, gsF, r_gsF):
    for n, (shi, sci) in enumerate(pairs):
        S.op("dve", lambda e, n=n, sci=sci: e.scalar_tensor_tensor(out=gsF[:, n, :], in0=vecF[:, sci, :], scalar=1.0, in1=vecF[:, gi, :], op0=ALU.add, op1=ALU.mult),
             reads=[r_vecF], writes=[r_gsF])


def build_ffn():
    nc = bass.Bass("TRN2", target_bir_lowering=False)
    x = nc.dram_tensor("x", [RT, 1024], F32, kind="ExternalInput").ap()
    vecF_d = nc.dram_tensor("vecF", [128, 5, 8], F32, kind="ExternalInput").ap()
    vecT_d = nc.dram_tensor("vecT", [2, 1024], F32, kind="ExternalInput").ap()
    w1 = nc.dram_tensor("w1", [1024, 4096], F32, kind="ExternalInput").ap()
    w2 = nc.dram_tensor("w2", [4096, 1024], F32, kind="ExternalInput").ap()
    out = nc.dram_tensor("out", [RT, 1024], F32, kind="ExternalOutput").ap()
    with ExitStack() as st:
        S = Sched(nc, st)
        ident, r_ident = _make_ident(S, BF16, "ident")
        vecF = S.sb("vecF_sb", [128, 5, 8], F32); r_vecF = S.res()
        gsF = S.sb("gsF", [128, 2, 8], F32); r_gsF = S.res()
        gate = S.sb("gate", [128, 2, 1024], F32); r_gate = S.res()
        S.dma("sp", "vec1", vecF[:], vecF_d, writes=[r_vecF])
        for s in range(2):
            S.dma("sp", "vec2", gate[:, s, :], vecT_d[s:s + 1, :].partition_broadcast(128), writes=[r_gate])
        _gs_prologue(S, vecF, r_vecF, 0, [(1, 2), (3, 4)], gsF, r_gsF)
        stg = [S.sb("stg%d" % i, [128, 2048], F32) for i in range(2)]
        r_stg = [S.res() for _ in range(2)]
        w1b = S.sb("w1b", [128, 8, 4096], BF16); r_w1b = S.res()
        w2b = S.sb("w2b", [128, 32, 1024], BF16); r_w2b = S.res()
        _load_weight_bf16(S, w1, w1b, 8, 4096, stg, r_stg, r_w1b, ["pool", "dve"], "w")
        _load_weight_bf16(S, w2, w2b, 32, 1024, stg, r_stg, r_w2b, ["pool", "dve"], "w")
        S.barrier()
        fr = Front(S, ident, r_ident)
        xt = [S.sb("xt%d" % i, [128, 2, 1024], F32) for i in range(2)]
        r_xt = [S.res() for _ in range(2)]
        hT = S.sb("hT", [128, 8, 256], BF16); r_hT = S.res()
        h1p = [S.ps("h1p%d" % i, [128, 256]) for i in range(2)]
        r_h1p = [S.res() for _ in range(2)]
        rl = [S.sb("rl%d" % i, [128, 256], F32) for i in range(2)]
        r_rl = [S.res() for _ in range(2)]
        h1T = [S.sb("h1T%d" % i, [128, 256], BF16) for i in range(2)]
        r_h1T = [S.res() for _ in range(2)]
        yp = [S.ps("yp%d" % i, [128, 512]) for i in range(4)]
        r_yp = [S.res() for _ in range(4)]
        tmp = S.sb("tmp", [128, 512], F32); r_tmp = S.res()
        tiles = [(i * 256, 2, 0) for i in range(RX // 256)] + [(RX, 1, 1)]
        last = []
        for ti, (r0, nsub, seg) in enumerate(tiles):
            b = ti % 2
            nt = nsub * 128
            S.dma("sp", "xt%d" % b, xt[b][:, :nsub, :], x[r0:r0 + nt, :].rearrange("(s p) d -> p s d", p=128), writes=[r_xt[b]])
            for s in range(nsub):
                fr.run(xt[b][:, s, :], r_xt[b], gsF[:, seg, :], vecF[:, 1 + 2 * seg, :], r_gsF, hT[:, :, s * 128:(s + 1) * 128], r_hT)
            for ffc in range(32):
                pb = ffc % 2
                for kc in range(8):
                    S.op("pe", lambda e, pb=pb, kc=kc, ffc=ffc, nt=nt: e.matmul(h1p[pb][:, :nt], lhsT=w1b[:, kc, ffc * 128:(ffc + 1) * 128], rhs=hT[:, kc, :nt], start=(kc == 0), stop=(kc == 7)),
                         reads=[r_w1b, r_hT], writes=[r_h1p[pb]])
                S.op("act", lambda e, pb=pb, nt=nt: e.activation(out=rl[pb][:, :nt], in_=h1p[pb][:, :nt], func=AF.Relu), reads=[r_h1p[pb]], writes=[r_rl[pb]])
                S.op("pool", lambda e, pb=pb, nt=nt: e.tensor_tensor(out=h1T[pb][:, :nt], in0=rl[pb][:, :nt], in1=rl[pb][:, :nt], op=ALU.mult), reads=[r_rl[pb]], writes=[r_h1T[pb]])
                for s in range(nsub):
                    for hf in range(2):
                        a = s * 2 + hf
                        S.op("pe", lambda e, pb=pb, a=a, s=s, hf=hf, ffc=ffc: e.matmul(yp[a][:], lhsT=h1T[pb][:, s * 128:(s + 1) * 128], rhs=w2b[:, ffc, hf * 512:(hf + 1) * 512], start=(ffc == 0), stop=(ffc == 31)),
                             reads=[r_h1T[pb], r_w2b], writes=[r_yp[a]])
            for s in range(nsub):
                for hf in range(2):
                    a = s * 2 + hf
                    S.op("dve", lambda e, a=a, hf=hf, seg=seg: e.tensor_tensor(out=tmp[:], in0=yp[a][:], in1=gate[:, seg, hf * 512:(hf + 1) * 512], op=ALU.mult), reads=[r_yp[a], r_gate], writes=[r_tmp])
                    S.op("dve", lambda e, b=b, s=s, hf=hf: e.tensor_tensor(out=xt[b][:, s, hf * 512:(hf + 1) * 512], in0=xt[b][:, s, hf * 512:(hf + 1) * 512], in1=tmp[:], op=ALU.add), reads=[r_tmp, r_xt[b]], writes=[r_xt[b]])
            r_o = S.res()
            S.dma("sp", "out%d" % b, out[r0:r0 + nt, :].rearrange("(s p) d -> p s d", p=128), xt[b][:, :nsub, :], reads=[r_xt[b]], writes=[r_o])
            last.append(r_o)
        S.final_wait("sp", last[-2:])
        S.emit()
    return nc


def build_posta(VD, HD, has_gate, NDIR):
    nc = bass.Bass("TRN2", target_bir_lowering=False)
    H = VD // HD
    NC_ = VD // 128
    x = nc.dram_tensor("x", [RT, 1024], F32, kind="ExternalInput").ap()
    o = nc.dram_tensor("o", [RT, NDIR, VD], F32, kind="ExternalInput").ap()
    vecF_d = nc.dram_tensor("vecF", [128, 5, 8], F32, kind="ExternalInput").ap()
    vecT_d = nc.dram_tensor("vecT", [2, 1024], F32, kind="ExternalInput").ap()
    ng_d = nc.dram_tensor("ng", [1, VD], F32, kind="ExternalInput").ap()
    wz = nc.dram_tensor("wz", [1024, VD], F32, kind="ExternalInput").ap()
    wout = nc.dram_tensor("wout", [VD, 1024], F32, kind="ExternalInput").ap()
    out = nc.dram_tensor("out", [RT, 1024], F32, kind="ExternalOutput").ap()
    with ExitStack() as st:
        S = Sched(nc, st)
        ident, r_ident = _make_ident(S, BF16, "ident")
        vecF = S.sb("vecF_sb", [128, 5, 8], F32); r_vecF = S.res()
        gsF = S.sb("gsF", [128, 2, 8], F32); r_gsF = S.res()
        gate = S.sb("gate", [128, 2, 1024], F32); r_gate = S.res()
        ng = S.sb("ng_sb", [128, VD], F32); r_ng = S.res()
        S.dma("sp", "vec3", vecF[:], vecF_d, writes=[r_vecF])
        for s in range(2):
            S.dma("sp", "vec4", gate[:, s, :], vecT_d[s:s + 1, :].partition_broadcast(128), writes=[r_gate])
        S.dma("sp", "vec5", ng[:], ng_d[0:1, :].partition_broadcast(128), writes=[r_ng])
        _gs_prologue(S, vecF, r_vecF, 0, [(1, 2), (3, 4)], gsF, r_gsF)
        stg = [S.sb("stg%d" % i, [128, 2048], F32) for i in range(2)]
        r_stg = [S.res() for _ in range(2)]
        woutb = S.sb("woutb", [128, NC_, 1024], BF16); r_woutb = S.res()
        if has_gate:
            wzb = S.sb("wzb", [128, 8, VD], BF16); r_wzb = S.res()
            _load_weight_bf16(S, wz, wzb, 8, VD, stg, r_stg, r_wzb, ["pool", "dve"], "w")
            fr = Front(S, ident, r_ident)
            hT = S.sb("hT", [128, 8, 128], BF16); r_hT = S.res()
            zp = [S.ps("zp%d" % i, [128, 512]) for i in range(2)]
            r_zp = [S.res() for _ in range(2)]
            sz = S.sb("sz", [128, VD], F32); r_sz = S.res()
            ssq = S.sb("ssq", [128, H], F32); r_ssq = S.res()
        _load_weight_bf16(S, wout, woutb, NC_, 1024, stg, r_stg, r_woutb, ["pool", "dve"], "w")
        xt = [S.sb("xt%d" % i, [128, 1024], F32) for i in range(2)]
        r_xt = [S.res() for _ in range(2)]
        ot = [S.sb("ot%d" % i, [128, NDIR, VD], F32) for i in range(2)]
        r_ot = [S.res() for _ in range(2)]
        og = S.sb("og", [128, VD], BF16); r_og = S.res()
        ogT = S.sb("ogT", [128, NC_, 128], BF16); r_ogT = S.res()
        tp = S.ps("tp", [128, NC_, 128], BF16); r_tp = S.res()
        yp = [S.ps("yp%d" % i, [128, 512]) for i in range(2)]
        r_yp = [S.res() for _ in range(2)]
        tmp = S.sb("tmp", [128, 512], F32); r_tmp = S.res()
        last = []
        S.barrier()
        for ti in range(RT // 128):
            seg = 1 if ti == RT // 128 - 1 else 0
            b = ti % 2
            r0 = ti * 128
            S.dma("sp", "xt%d" % b, xt[b][:], x[r0:r0 + 128, :], writes=[r_xt[b]])
            S.dma("sp", "ot%d" % b, ot[b][:], o[r0:r0 + 128, :, :], writes=[r_ot[b]])
            if has_gate:
                fr.run(xt[b][:], r_xt[b], gsF[:, seg, :], vecF[:, 1 + 2 * seg, :], r_gsF, hT[:], r_hT)
                for blk in range(VD // 512):
                    pb = blk % 2
                    for kc in range(8):
                        S.op("pe", lambda e, pb=pb, kc=kc, blk=blk: e.matmul(zp[pb][:], lhsT=hT[:, kc, :], rhs=wzb[:, kc, blk * 512:(blk + 1) * 512], start=(kc == 0), stop=(kc == 7)),
                             reads=[r_hT, r_wzb], writes=[r_zp[pb]])
                    S.op("act", lambda e, pb=pb, blk=blk: e.activation(out=sz[:, blk * 512:(blk + 1) * 512], in_=zp[pb][:], func=AF.Silu), reads=[r_zp[pb]], writes=[r_sz])
                o0 = ot[b][:, 0, :]
                o1 = ot[b][:, NDIR - 1, :]
                if NDIR == 2:
                    S.op("pool", lambda e, o0=o0, o1=o1: e.tensor_tensor(out=o0, in0=o0, in1=o1, op=ALU.add), reads=[r_ot[b]], writes=[r_ot[b]])
                    sq = o1
                else:
                    sq = tmp_big[:]
                S.op("pool", lambda e, o0=o0, sq=sq: e.tensor_tensor(out=sq, in0=o0, in1=o0, op=ALU.mult), reads=[r_ot[b]], writes=[r_ot[b]])
                S.op("dve", lambda e, sq=sq: e.tensor_reduce(out=ssq[:], in_=sq.rearrange("p (h d) -> p h d", h=H), axis=AX.X, op=ALU.add), reads=[r_ot[b]], writes=[r_ssq])
                S.op("dve", lambda e: e.tensor_scalar(out=ssq[:], in0=ssq[:], scalar1=1.0 / HD, scalar2=EPS, op0=ALU.mult, op1=ALU.add), reads=[r_ssq], writes=[r_ssq])
                S.op("act", lambda e: e.activation(out=ssq[:], in_=ssq[:], func=AF.Sqrt), reads=[r_ssq], writes=[r_ssq])
                S.op("dve", lambda e: e.reciprocal(out=ssq[:], in_=ssq[:]), reads=[r_ssq], writes=[r_ssq])
                S.op("dve", lambda e, o0=o0: e.tensor_tensor(out=o0.rearrange("p (h d) -> p h d", h=H), in0=o0.rearrange("p (h d) -> p h d", h=H), in1=ssq[:].unsqueeze(2).to_broadcast([128, H, HD]), op=ALU.mult),
                     reads=[r_ot[b], r_ssq], writes=[r_ot[b]])
                S.op("pool", lambda e, o0=o0: e.tensor_tensor(out=o0, in0=o0, in1=ng[:], op=ALU.mult), reads=[r_ot[b], r_ng], writes=[r_ot[b]])
                S.op("dve", lambda e, o0=o0: e.tensor_tensor(out=og[:], in0=o0, in1=sz[:], op=ALU.mult), reads=[r_ot[b], r_sz], writes=[r_og])
            else:
                S.op("act", lambda e, b=b: e.copy(out=og[:], in_=ot[b][:, 0, :]), reads=[r_ot[b]], writes=[r_og])
            for c in range(NC_):
                S.op("pe", lambda e, c=c: e.transpose(out=tp[:, c, :], in_=og[:, c * 128:(c + 1) * 128], identity=ident[:]), reads=[r_og, r_ident], writes=[r_tp])
            S.op("act", lambda e: e.copy(out=ogT[:], in_=tp[:]), reads=[r_tp], writes=[r_ogT])
            for hf in range(2):
                for c in range(NC_):
                    S.op("pe", lambda e, hf=hf, c=c: e.matmul(yp[hf][:], lhsT=ogT[:, c, :], rhs=woutb[:, c, hf * 512:(hf + 1) * 512], start=(c == 0), stop=(c == NC_ - 1)),
                         reads=[r_ogT, r_woutb], writes=[r_yp[hf]])
                S.op("dve", lambda e, hf=hf, seg=seg: e.tensor_tensor(out=tmp[:], in0=yp[hf][:], in1=gate[:, seg, hf * 512:(hf + 1) * 512], op=ALU.mult), reads=[r_yp[hf], r_gate], writes=[r_tmp])
                S.op("dve", lambda e, b=b, hf=hf: e.tensor_tensor(out=xt[b][:, hf * 512:(hf + 1) * 512], in0=xt[b][:, hf * 512:(hf + 1) * 512], in1=tmp[:], op=ALU.add), reads=[r_tmp, r_xt[b]], writes=[r_xt[b]])
            r_o = S.res()
            S.dma("sp", "out%d" % b, out[r0:r0 + 128, :], xt[b][:], reads=[r_xt[b]], writes=[r_o])
            last.append(r_o)
        S.final_wait("sp", last[-2:])
        S.emit()
    return nc


NSEQ = 16384 + 256


def _rsqrt_ops(S, t_ap, r_t, mul, add):
    S.op("dve", lambda e: e.tensor_scalar(out=t_ap, in0=t_ap, scalar1=mul, scalar2=add, op0=ALU.mult, op1=ALU.add), reads=[r_t], writes=[r_t])
    S.op("act", lambda e: e.activation(out=t_ap, in_=t_ap, func=AF.Sqrt), reads=[r_t], writes=[r_t])
    S.op("dve", lambda e: e.reciprocal(out=t_ap, in_=t_ap), reads=[r_t], writes=[r_t])


def build_attn(ntiles_x=128, n_ctx_tiles=2):
    nc = bass.Bass("TRN2", target_bir_lowering=False)
    NT = ntiles_x + n_ctx_tiles
    NS = NT * 128
    xs = nc.dram_tensor("xs", [NS, 1024], F32, kind="ExternalInput").ap()
    vecF_d = nc.dram_tensor("vecF", [128, 5, 8], F32, kind="ExternalInput").ap()
    win = nc.dram_tensor("win", [1024, 512], F32, kind="ExternalInput").ap()
    gn_d = nc.dram_tensor("gn", [1, 384], F32, kind="ExternalInput").ap()
    tab_d = nc.dram_tensor("tab", [NS, 128], F32, kind="ExternalInput").ap()
    out = nc.dram_tensor("out", [NS, 256], F32, kind="ExternalOutput").ap()
    SCALE = 128.0 ** -0.5
    with ExitStack() as st:
        S = Sched(nc, st)
        ident, r_ident = _make_ident(S, BF16, "ident")
        bank = [S.ps("bank%d" % i, [128, 512]) for i in range(6)]
        r_bank = [S.res() for _ in range(6)]
        vecF = S.sb("vecF_sb", [128, 5, 8], F32); r_vecF = S.res()
        gsF = S.sb("gsF", [128, 2, 8], F32); r_gsF = S.res()
        gn = S.sb("gn_sb", [128, 384], F32); r_gn = S.res()
        S.dma("sp", "vec6", vecF[:], vecF_d, writes=[r_vecF])
        S.dma("sp", "vec7", gn[:], gn_d[0:1, :].partition_broadcast(128), writes=[r_gn])
        _gs_prologue(S, vecF, r_vecF, 0, [(1, 2), (3, 4)], gsF, r_gsF)
        stg = [S.sb("stg%d" % i, [128, 512], F32) for i in range(2)]
        r_stg = [S.res() for _ in range(2)]
        winb = S.sb("winb", [128, 8, 512], BF16); r_winb = S.res()
        _load_weight_bf16(S, win, winb, 8, 512, stg, r_stg, r_winb, ["pool", "dve"], "w")
        QKT = S.sb("QKT", [128, 3, NS], BF16); r_QKT = S.res()
        Vx = S.sb("Vx", [128, NT, 130], BF16); r_Vx = S.res()
        S.op("pool", lambda e: e.memset(Vx[:, :, 128:130], 1.0), writes=[r_Vx])
        fr = Front(S, ident, r_ident)
        xt = [S.sb("xt%d" % i, [128, 1024], F32) for i in range(2)]
        r_xt = [S.res() for _ in range(2)]
        tb = [S.sb("tb%d" % i, [128, 128], F32) for i in range(2)]
        r_tb = [S.res() for _ in range(2)]
        hT = S.sb("hT", [128, 8, 128], BF16); r_hT = S.res()
        sq = S.sb("sq", [128, 384], F32); r_sq = S.res()
        ssq = S.sb("ssq", [128, 3], F32); r_ssq = S.res()
        qn = S.sb("qn", [128, 384], F32); r_qn = S.res()
        rA = S.sb("rA", [128, 192], F32); r_rA = S.res()
        rB = S.sb("rB", [128, 192], F32); r_rB = S.res()
        rC = S.sb("rC", [128, 192], F32); r_rC = S.res()
        rD = S.sb("rD", [128, 192], F32); r_rD = S.res()
        qr = S.sb("qr", [128, 384], BF16); r_qr = S.res()
        PB, TB_ = 0, 1
        tpv = bank[TB_][:].bitcast(BF16)

        def v5(t):
            return t.rearrange("p (h a t d) -> p h a t d", h=3, a=2, t=2)

        def v4(t):
            return t.rearrange("p (h a d) -> p h a d", h=3, a=2)

        S.barrier()
        for ti in range(NT):
            seg = 1 if ti >= ntiles_x else 0
            b = ti % 2
            S.dma("sp", "xt%d" % b, xt[b][:], xs[ti * 128:(ti + 1) * 128, :], writes=[r_xt[b]])
            S.dma("sp", "tb%d" % b, tb[b][:], tab_d[ti * 128:(ti + 1) * 128, :], writes=[r_tb[b]])
            fr.run(xt[b][:], r_xt[b], gsF[:, seg, :], vecF[:, 1 + 2 * seg, :], r_gsF, hT[:], r_hT)
            for kc in range(8):
                S.op("pe", lambda e, kc=kc: e.matmul(bank[PB][:], lhsT=hT[:, kc, :], rhs=winb[:, kc, :], start=(kc == 0), stop=(kc == 7)), reads=[r_hT, r_winb], writes=[r_bank[PB]])
            S.op("act", lambda e: e.activation(out=sq[:], in_=bank[PB][:, 0:384], func=AF.Square), reads=[r_bank[PB]], writes=[r_sq])
            S.op("act", lambda e, ti=ti: e.copy(out=Vx[:, ti, 0:128], in_=bank[PB][:, 384:512]), reads=[r_bank[PB]], writes=[r_Vx])
            S.op("dve", lambda e: e.tensor_reduce(out=ssq[:], in_=sq[:].rearrange("p (h d) -> p h d", h=3), axis=AX.X, op=ALU.add), reads=[r_sq], writes=[r_ssq])
            _rsqrt_ops(S, ssq[:], r_ssq, 1.0 / 128, EPS)
            S.op("dve", lambda e: e.tensor_tensor(out=qn[:].rearrange("p (h d) -> p h d", h=3), in0=bank[PB][:, 0:384].rearrange("p (h d) -> p h d", h=3), in1=ssq[:].unsqueeze(2).to_broadcast([128, 3, 128]), op=ALU.mult),
                 reads=[r_bank[PB], r_ssq], writes=[r_qn])
            S.op("pool", lambda e: e.tensor_tensor(out=qn[:], in0=qn[:], in1=gn[:], op=ALU.mult), reads=[r_qn, r_gn], writes=[r_qn])
            tbv = tb[b][:].rearrange("p (a t d) -> p a t d", a=2, t=2)
            cosb = tbv[:, :, 0, :].unsqueeze(1).to_broadcast([128, 3, 2, 32])
            sinb = tbv[:, :, 1, :].unsqueeze(1).to_broadcast([128, 3, 2, 32])
            x1 = v5(qn[:])[:, :, :, 0, :]
            x2 = v5(qn[:])[:, :, :, 1, :]
            S.op("dve", lambda e, x1=x1, cosb=cosb: e.tensor_tensor(out=v4(rA[:]), in0=x1, in1=cosb, op=ALU.mult), reads=[r_qn, r_tb[b]], writes=[r_rA])
            S.op("pool", lambda e, x2=x2, sinb=sinb: e.tensor_tensor(out=v4(rB[:]), in0=x2, in1=sinb, op=ALU.mult), reads=[r_qn, r_tb[b]], writes=[r_rB])
            S.op("dve", lambda e, x1=x1, sinb=sinb: e.tensor_tensor(out=v4(rC[:]), in0=x1, in1=sinb, op=ALU.mult), reads=[r_qn, r_tb[b]], writes=[r_rC])
            S.op("pool", lambda e, x2=x2, cosb=cosb: e.tensor_tensor(out=v4(rD[:]), in0=x2, in1=cosb, op=ALU.mult), reads=[r_qn, r_tb[b]], writes=[r_rD])
            S.op("dve", lambda e: e.tensor_tensor(out=v5(qr[:])[:, :, :, 0, :], in0=v4(rA[:]), in1=v4(rB[:]), op=ALU.subtract), reads=[r_rA, r_rB], writes=[r_qr])
            S.op("pool", lambda e: e.tensor_tensor(out=v5(qr[:])[:, :, :, 1, :], in0=v4(rC[:]), in1=v4(rD[:]), op=ALU.add), reads=[r_rC, r_rD], writes=[r_qr])
            for h in range(3):
                S.op("pe", lambda e, h=h: e.transpose(out=tpv[:, h * 128:(h + 1) * 128], in_=qr[:, h * 128:(h + 1) * 128], identity=ident[:]), reads=[r_qr, r_ident], writes=[r_bank[TB_]])
            S.op("act", lambda e, ti=ti: e.copy(out=QKT[:, :, ti * 128:(ti + 1) * 128], in_=tpv[:, 0:384].rearrange("p (h t) -> p h t", h=3)), reads=[r_bank[TB_]], writes=[r_QKT])
        pT = [S.sb("pT%d" % i, [128, 512], BF16) for i in range(2)]
        r_pT = [S.res() for _ in range(2)]
        rc = S.sb("rc", [128, 4], F32); r_rc = S.res()
        ob = [S.sb("ob%d" % i, [128, 4, 128], F32) for i in range(2)]
        r_ob = [S.res() for _ in range(2)]
        last = []
        jobs = [(h, q0, 512, list(range(NT))) for h in range(2) for q0 in range(0, ntiles_x * 128, 512)]
        jobs += [(h, ntiles_x * 128, n_ctx_tiles * 128, list(range(ntiles_x, NT))) for h in range(2)]
        it = 0
        for ji, (h, q0, nq, kts) in enumerate(jobs):
            nsub = nq // 128
            aset = ji % 2
            accb = [2 + 2 * aset, 3 + 2 * aset]

            def acc(sub):
                return bank[accb[sub // 2]][:, (sub % 2) * 256:(sub % 2) * 256 + 129]

            for ki, kt in enumerate(kts):
                sb_ = it % 2
                it += 1
                S.op("pe", lambda e, sb_=sb_, kt=kt, h=h, q0=q0, nq=nq: e.matmul(bank[sb_][:, :nq], lhsT=QKT[:, 2, kt * 128:(kt + 1) * 128], rhs=QKT[:, h, q0:q0 + nq], start=True, stop=True),
                     reads=[r_QKT], writes=[r_bank[sb_]])
                S.op("act", lambda e, sb_=sb_, nq=nq: e.activation(out=pT[sb_][:, :nq], in_=bank[sb_][:, :nq], func=AF.Exp, scale=SCALE), reads=[r_bank[sb_]], writes=[r_pT[sb_]])
                for sub in range(nsub):
                    S.op("pe", lambda e, sb_=sb_, sub=sub, kt=kt, ki=ki, n=len(kts), a=acc(sub): e.matmul(a, lhsT=pT[sb_][:, sub * 128:(sub + 1) * 128], rhs=Vx[:, kt, 0:129], start=(ki == 0 and sub % 2 == 0), stop=(ki == n - 1), skip_group_check=True),
                         reads=[r_pT[sb_], r_Vx], writes=[r_bank[accb[sub // 2]]])
            o_b = ji % 2
            for sub in range(nsub):
                a = acc(sub)
                S.op("dve", lambda e, a=a, sub=sub: e.reciprocal(out=rc[:, sub:sub + 1], in_=a[:, 128:129]), reads=[r_bank[accb[sub // 2]]], writes=[r_rc])
                S.op("dve", lambda e, a=a, sub=sub, o_b=o_b: e.tensor_scalar(out=ob[o_b][:, sub, :], in0=a[:, 0:128], scalar1=rc[:, sub:sub + 1], scalar2=None, op0=ALU.mult), reads=[r_bank[accb[sub // 2]], r_rc], writes=[r_ob[o_b]])
            r_o = S.res()
            S.dma("sp", "out%d" % o_b, out[q0:q0 + nq, h * 128:(h + 1) * 128].rearrange("(s p) d -> p s d", p=128), ob[o_b][:, :nsub, :], reads=[r_ob[o_b]], writes=[r_o])
            last.append(r_o)
        S.final_wait("sp", last[-2:])
        S.emit()
    return nc


def _make_tri(S, name, ge=True):
    t = S.sb(name, [128, 128], F32)
    r = S.res()
    S.op("pool", lambda e: e.memset(t[:], 1.0), writes=[r])
    S.op("pool", lambda e: e.affine_select(out=t[:], in_=t[:], pattern=[[1, 128]], compare_op=(ALU.is_ge if ge else ALU.is_gt), fill=0.0, base=0, channel_multiplier=-1), reads=[r], writes=[r])
    return t, r


def build_gla(nch_ctx=2, nch_x=128):
    nc = bass.Bass("TRN2", target_bir_lowering=False)
    NCH = nch_ctx + nch_x
    NS = NCH * 128
    xs = nc.dram_tensor("xs", [NS, 1024], F32, kind="ExternalInput").ap()
    vecF_d = nc.dram_tensor("vecF", [128, 5, 8], F32, kind="ExternalInput").ap()
    win = nc.dram_tensor("win", [1024, 1040], F32, kind="ExternalInput").ap()
    gw2_d = nc.dram_tensor("gw2", [16, 256], F32, kind="ExternalInput").ap()
    gb2_d = nc.dram_tensor("gb2", [1, 256], F32, kind="ExternalInput").ap()
    out = nc.dram_tensor("out", [NS, 512], F32, kind="ExternalOutput").ap()
    QS = 128.0 ** -0.5
    with ExitStack() as st:
        S = Sched(nc, st)
        ident, r_ident = _make_ident(S, BF16, "ident")
        U, r_U = _make_tri(S, "U")
        vecF = S.sb("vecF_sb", [128, 5, 8], F32); r_vecF = S.res()
        gsF = S.sb("gsF", [128, 2, 8], F32); r_gsF = S.res()
        gw2 = S.sb("gw2_sb", [16, 256], F32); r_gw2 = S.res()
        gb2 = S.sb("gb2_sb", [128, 256], F32); r_gb2 = S.res()
        S.dma("sp", "vec8", vecF[:], vecF_d, writes=[r_vecF])
        S.dma("sp", "vec9", gw2[:], gw2_d, writes=[r_gw2])
        S.dma("sp", "vec10", gb2[:], gb2_d[0:1, :].partition_broadcast(128), writes=[r_gb2])
        _gs_prologue(S, vecF, r_vecF, 0, [(1, 2), (3, 4)], gsF, r_gsF)
        stg = [S.sb("stg%d" % i, [128, 1040], F32) for i in range(2)]
        r_stg = [S.res() for _ in range(2)]
        winb = S.sb("winb", [128, 8, 1040], BF16); r_winb = S.res()
        _load_weight_bf16(S, win, winb, 8, 1040, stg, r_stg, r_winb, ["pool", "dve"], "w")
        fr = Front(S, ident, r_ident)
        pQK = S.ps("pQK", [128, 4, 128]); r_pQK = S.res()
        pL = S.ps("pL", [128, 512]); r_pL = S.res()
        pV = S.ps("pV", [128, 512]); r_pV = S.res()
        pC = S.ps("pC", [128, 2, 128]); r_pC = S.res()
        pA = S.ps("pA", [128, 2, 128]); r_pA = S.res()
        pO = S.ps("pO", [128, 2, 256]); r_pO = S.res()
        pS = S.ps("pS", [128, 2, 256]); r_pS = S.res()
        kdT_ps = pL[:, 384:512].bitcast(BF16).rearrange("p (h t) -> p h t", h=2)
        xt = [S.sb("xt%d" % i, [128, 1024], F32) for i in range(2)]
        r_xt = [S.res() for _ in range(2)]
        hT = S.sb("hT", [128, 8, 128], BF16); r_hT = S.res()
        lowT = S.sb("lowT", [16, 128], F32); r_lowT = S.res()
        lg = S.sb("lg", [128, 256], F32); r_lg = S.res()
        Vb = S.sb("Vb", [128, 512], BF16); r_Vb = S.res()
        cb = S.sb("cb", [128, 2, 4], F32); r_cb = S.res()
        ex = S.sb("ex", [128, 2, 4, 128], F32); r_ex = S.res()
        qt = S.sb("qt", [128, 2, 128], BF16); r_qt = S.res()
        kt_ = S.sb("kt", [128, 2, 128], BF16); r_kt = S.res()
        qe = S.sb("qe", [128, 2, 128], BF16); r_qe = S.res()
        kdT = S.sb("kdT", [128, 2, 128], BF16); r_kdT = S.res()
        kd = S.sb("kd", [128, 2, 128], BF16); r_kd = S.res()
        Aq = S.sb("Aq", [128, 2, 128], BF16); r_Aq = S.res()
        St = S.sb("St", [128, 2, 256], F32); r_St = S.res()
        Sb = S.sb("Sb", [128, 2, 256], BF16); r_Sb = S.res()
        ob = [S.sb("ob%d" % i, [128, 512], F32) for i in range(2)]
        r_ob = [S.res() for _ in range(2)]
        S.op("pool", lambda e: e.memset(St[:], 0.0), writes=[r_St])
        S.op("pool", lambda e: e.memset(Sb[:], 0.0), writes=[r_Sb])
        last = []
        S.barrier()
        for c in range(NCH):
            seg = 1 if c < nch_ctx else 0
            b = c % 2
            S.dma("sp", "xt%d" % b, xt[b][:], xs[c * 128:(c + 1) * 128, :], writes=[r_xt[b]])
            fr.run(xt[b][:], r_xt[b], gsF[:, seg, :], vecF[:, 1 + 2 * seg, :], r_gsF, hT[:], r_hT)
            for m in range(4):
                for kc in range(8):
                    S.op("pe", lambda e, m=m, kc=kc: e.matmul(pQK[:, m, :], lhsT=winb[:, kc, m * 128:(m + 1) * 128], rhs=hT[:, kc, :], start=(kc == 0), stop=(kc == 7)), reads=[r_winb, r_hT], writes=[r_pQK])
            for kc in range(8):
                S.op("pe", lambda e, kc=kc: e.matmul(pV[:], lhsT=hT[:, kc, :], rhs=winb[:, kc, 512:1024], start=(kc == 0), stop=(kc == 7)), reads=[r_winb, r_hT], writes=[r_pV])
            for kc in range(8):
                S.op("pe", lambda e, kc=kc: e.matmul(pL[0:16, 0:128], lhsT=winb[:, kc, 1024:1040], rhs=hT[:, kc, :], start=(kc == 0), stop=(kc == 7)), reads=[r_winb, r_hT], writes=[r_pL])
            S.op("act", lambda e: e.copy(out=Vb[:], in_=pV[:]), reads=[r_pV], writes=[r_Vb])
            S.op("act", lambda e: e.copy(out=lowT[:], in_=pL[0:16, 0:128]), reads=[r_pL], writes=[r_lowT])
            S.op("pe", lambda e: e.matmul(pL[:, 128:384], lhsT=lowT[:], rhs=gw2[:], start=True, stop=True), reads=[r_lowT, r_gw2], writes=[r_pL])
            S.op("dve", lambda e: e.tensor_tensor(out=lg[:], in0=pL[:, 128:384], in1=gb2[:], op=ALU.add), reads=[r_pL, r_gb2], writes=[r_lg])
            S.op("act", lambda e: e.activation(out=lg[:], in_=lg[:], func=AF.Exp, scale=-1.0), reads=[r_lg], writes=[r_lg])
            S.op("act", lambda e: e.activation(out=lg[:], in_=lg[:], func=AF.Ln, bias=1.0), reads=[r_lg], writes=[r_lg])
            for h in range(2):
                S.op("pe", lambda e, h=h: e.matmul(pC[:, h, :], lhsT=lg[:, h * 128:(h + 1) * 128], rhs=U[:], start=True, stop=True), reads=[r_lg, r_U], writes=[r_pC])
            for h in range(2):
                S.op("dve", lambda e, h=h: e.tensor_scalar(out=cb[:, h, 0:1], in0=pC[:, h, 63:64], scalar1=1.0 / 16, scalar2=None, op0=ALU.mult), reads=[r_pC], writes=[r_cb])
                S.op("dve", lambda e, h=h: e.tensor_scalar(out=cb[:, h, 1:2], in0=pC[:, h, 63:64], scalar1=-1.0 / 16, scalar2=None, op0=ALU.mult), reads=[r_pC], writes=[r_cb])
                S.op("dve", lambda e, h=h: e.tensor_scalar(out=cb[:, h, 2:3], in0=pC[:, h, 127:128], scalar1=-1.0 / 16, scalar2=None, op0=ALU.mult), reads=[r_pC], writes=[r_cb])
                S.op("act", lambda e, h=h: e.activation(out=ex[:, h, 0, :], in_=pC[:, h, :], func=AF.Exp, scale=-1.0 / 16, bias=cb[:, h, 0:1]), reads=[r_pC, r_cb], writes=[r_ex])
                S.op("act", lambda e, h=h: e.activation(out=ex[:, h, 1, :], in_=pC[:, h, :], func=AF.Exp, scale=1.0 / 16, bias=cb[:, h, 1:2]), reads=[r_pC, r_cb], writes=[r_ex])
                S.op("act", lambda e, h=h: e.activation(out=ex[:, h, 2, :], in_=pC[:, h, :], func=AF.Exp, scale=-1.0 / 16), reads=[r_pC], writes=[r_ex])
                S.op("act", lambda e, h=h: e.activation(out=ex[:, h, 3, :], in_=pC[:, h, :], func=AF.Exp, scale=1.0 / 16, bias=cb[:, h, 2:3]), reads=[r_pC, r_cb], writes=[r_ex])
                S.op("dve", lambda e, h=h: e.scalar_tensor_tensor(out=qt[:, h, :], in0=pQK[:, h, :], scalar=QS, in1=ex[:, h, 0, :], op0=ALU.mult, op1=ALU.mult), reads=[r_pQK, r_ex], writes=[r_qt])
                S.op("dve", lambda e, h=h: e.tensor_tensor(out=kt_[:, h, :], in0=pQK[:, 2 + h, :], in1=ex[:, h, 1, :], op=ALU.mult), reads=[r_pQK, r_ex], writes=[r_kt])
                S.op("dve", lambda e, h=h: e.scalar_tensor_tensor(out=qe[:, h, :], in0=pQK[:, h, :], scalar=QS, in1=ex[:, h, 2, :], op0=ALU.mult, op1=ALU.mult), reads=[r_pQK, r_ex], writes=[r_qe])
                S.op("dve", lambda e, h=h: e.tensor_tensor(out=kdT[:, h, :], in0=pQK[:, 2 + h, :], in1=ex[:, h, 3, :], op=ALU.mult), reads=[r_pQK, r_ex], writes=[r_kdT])
                S.op("pe", lambda e, h=h: e.transpose(out=kdT_ps[:, h, :], in_=kdT[:, h, :], identity=ident[:]), reads=[r_kdT, r_ident], writes=[r_pL])
                S.op("pe", lambda e, h=h: e.matmul(pA[:, h, :], lhsT=kt_[:, h, :], rhs=qt[:, h, :], start=True, stop=True), reads=[r_kt, r_qt], writes=[r_pA])
            S.op("act", lambda e: e.copy(out=kd[:], in_=kdT_ps), reads=[r_pL], writes=[r_kd])
            S.op("dve", lambda e: e.tensor_tensor(out=Aq[:], in0=pA[:], in1=U[:].unsqueeze(1).to_broadcast([128, 2, 128]), op=ALU.mult), reads=[r_pA, r_U], writes=[r_Aq])
            for h in range(2):
                S.op("pe", lambda e, h=h: e.matmul(pO[:, h, :], lhsT=Aq[:, h, :], rhs=Vb[:, h * 256:(h + 1) * 256], start=True, stop=False), reads=[r_Aq, r_Vb], writes=[r_pO])
                S.op("pe", lambda e, h=h: e.matmul(pO[:, h, :], lhsT=qe[:, h, :], rhs=Sb[:, h, :], start=False, stop=True), reads=[r_qe, r_Sb], writes=[r_pO])
            for h in range(2):
                S.op("pe", lambda e, h=h: e.matmul(pS[:, h, :], lhsT=kd[:, h, :], rhs=Vb[:, h * 256:(h + 1) * 256], start=True, stop=True), reads=[r_kd, r_Vb], writes=[r_pS])
            S.op("act", lambda e, b=b: e.copy(out=ob[b][:], in_=pO[:].rearrange("p h d -> p (h d)")), reads=[r_pO], writes=[r_ob[b]])
            for h in range(2):
                S.op("dve", lambda e, h=h: e.scalar_tensor_tensor(out=St[:, h, :], in0=St[:, h, :], scalar=ex[:, h, 2, 127:128], in1=pS[:, h, :], op0=ALU.mult, op1=ALU.add), reads=[r_St, r_ex, r_pS], writes=[r_St])
            S.op("act", lambda e: e.copy(out=Sb[:], in_=St[:]), reads=[r_St], writes=[r_Sb])
            r_o = S.res()
            S.dma("sp", "out%d" % b, out[c * 128:(c + 1) * 128, :], ob[b][:], reads=[r_ob[b]], writes=[r_o])
            last.append(r_o)
        S.final_wait("sp", last[-2:])
        S.emit()
    return nc


def build_dn(nch_ctx=2, nch_x=128):
    nc = bass.Bass("TRN2", target_bir_lowering=False)
    NCH = nch_ctx + nch_x
    NS = NCH * 128
    xs = nc.dram_tensor("xs", [NS, 1024], F32, kind="ExternalInput").ap()
    vecF_d = nc.dram_tensor("vecF", [128, 5, 8], F32, kind="ExternalInput").ap()
    win = nc.dram_tensor("win", [1024, 2064], F32, kind="ExternalInput").ap()
    cw_d = nc.dram_tensor("cw", [128, 16, 5], F32, kind="ExternalInput").ap()
    al_d = nc.dram_tensor("al", [1, 16], F32, kind="ExternalInput").ap()
    msk_d = nc.dram_tensor("msk", [128, 4, 128], F32, kind="ExternalInput").ap()
    out = nc.dram_tensor("out", [NS, 1024], F32, kind="ExternalOutput").ap()
    QS = 128.0 ** -0.5
    with ExitStack() as st:
        S = Sched(nc, st)
        P = [S.ps("P%d" % i, [128, 8, 128]) for i in range(3)]
        r_P = [S.res() for _ in range(3)]
        Bs = S.ps("Bs", [128, 512]); r_Bs = S.res()
        pctr = [0]

        def getP():
            i = pctr[0] % 3
            pctr[0] += 1
            return P[i], r_P[i]

        ident, r_ident = _make_ident(S, BF16, "ident")
        identF, r_identF = _make_ident(S, F32, "identF")
        U, r_U = _make_tri(S, "U", True)
        SU, r_SU = _make_tri(S, "SU", False)
        onesF = S.sb("onesF", [128, 128], F32); r_onesF = S.res()
        onesB = S.sb("onesB", [128, 128], BF16); r_onesB = S.res()
        S.op("pool", lambda e: e.memset(onesF[:], 1.0), writes=[r_onesF])
        S.op("pool", lambda e: e.memset(onesB[:], 1.0), writes=[r_onesB])
        vecF = S.sb("vecF_sb", [128, 5, 8], F32); r_vecF = S.res()
        gsF = S.sb("gsF", [128, 2, 8], F32); r_gsF = S.res()
        cw = S.sb("cw_sb", [128, 16, 5], F32); r_cw = S.res()
        al = S.sb("al_sb", [128, 16], F32); r_al = S.res()
        S.dma("sp", "vec11", vecF[:], vecF_d, writes=[r_vecF])
        S.dma("sp", "vec12", cw[:], cw_d, writes=[r_cw])
        S.dma("sp", "vec13", al[:], al_d[0:1, :].partition_broadcast(128), writes=[r_al])
        _gs_prologue(S, vecF, r_vecF, 0, [(1, 2), (3, 4)], gsF, r_gsF)
        S.op("act", lambda e: e.activation(out=al[:, 0:8], in_=al[:, 0:8], func=AF.Exp), reads=[r_al], writes=[r_al])
        S.op("dve", lambda e: e.tensor_scalar(out=al[:, 0:8], in0=al[:, 0:8], scalar1=-1.0, scalar2=None, op0=ALU.mult), reads=[r_al], writes=[r_al])
        stg = [S.sb("stg%d" % i, [128, 1032], F32) for i in range(2)]
        r_stg = [S.res() for _ in range(2)]
        winb = S.sb("winb", [128, 8, 2064], BF16); r_winb = S.res()
        for k in range(8):
            for hf in range(2):
                b = (k * 2 + hf) % 2
                S.dma("sp", "w_stg%d" % b, stg[b][:], win[k * 128:(k + 1) * 128, hf * 1032:(hf + 1) * 1032], writes=[r_stg[b]])
                S.op("pool" if b else "dve", lambda e, b=b, k=k, hf=hf: e.tensor_copy(out=winb[:, k, hf * 1032:(hf + 1) * 1032], in_=stg[b][:]), reads=[r_stg[b]], writes=[r_winb])
        fr = Front(S, ident, r_ident)
        xt = [S.sb("xt%d" % i, [128, 1024], F32) for i in range(2)]
        r_xt = [S.res() for _ in range(2)]
        hT = S.sb("hT", [128, 8, 128], BF16); r_hT = S.res()
        PC = [S.sb("PC%d" % i, [128, 16, 132], F32) for i in range(2)]
        r_PC = [S.res() for _ in range(2)]
        abt = [S.sb("abt%d" % i, [128, 16], F32) for i in range(2)]
        r_abt = [S.res() for _ in range(2)]
        cacc = S.sb("cacc", [128, 8, 128], F32); r_cacc = S.res()
        ctmp = S.sb("ctmp", [128, 8, 128], F32); r_ctmp = S.res()
        Yqk = S.sb("Yqk", [128, 8, 128], F32); r_Yqk = S.res()
        Vsl = S.sb("Vsl", [128, 8, 128], BF16); r_Vsl = S.res()
        sqb = S.sb("sqb", [128, 8, 128], BF16); r_sqb = S.res()
        rn = S.sb("rn", [128, 8, 128], F32); r_rn = S.res()
        QT = S.sb("QT", [128, 4, 128], BF16); r_QT = S.res()
        KT = S.sb("KT", [128, 4, 128], BF16); r_KT = S.res()
        Vt = S.sb("Vt", [128, 8, 128], BF16); r_Vt = S.res()
        Ktok = S.sb("Ktok", [128, 4, 128], BF16); r_Ktok = S.res()
        kd = S.sb("kd", [128, 8, 128], BF16); r_kd = S.res()
        gt = S.sb("gt", [128, 8, 8], F32); r_gt = S.res()
        kk = S.sb("kk", [128, 4, 128], F32); r_kk = S.res()
        qk = S.sb("qk", [128, 4, 128], F32); r_qk = S.res()
        dG = S.sb("dG", [128, 8, 128], F32); r_dG = S.res()
        DT = S.sb("DT", [128, 8, 128], F32); r_DT = S.res()
        nbSU = S.sb("nbSU", [128, 8, 128], F32); r_nbSU = S.res()
        t1 = S.sb("t1", [128, 8, 128], F32); r_t1 = S.res()
        Xb = [S.sb("Xb%d" % i, [128, 8, 128], F32) for i in range(2)]; r_Xb = [S.res() for _ in range(2)]
        Yb = [S.sb("Yb%d" % i, [128, 8, 128], F32) for i in range(2)]; r_Yb = [S.res() for _ in range(2)]
        Qx = [S.sb("Qx%d" % i, [128, 8, 128], F32) for i in range(2)]; r_Qx = [S.res() for _ in range(2)]
        Qy = [S.sb("Qy%d" % i, [128, 8, 128], F32) for i in range(2)]; r_Qy = [S.res() for _ in range(2)]
        TT = S.sb("TT", [128, 8, 128], BF16); r_TT = S.res()
        X0t = S.sb("X0t", [128, 8, 128], F32); r_X0t = S.res()
        Y0t = S.sb("Y0t", [128, 8, 128], F32); r_Y0t = S.res()
        msk = S.sb("msk_sb", [128, 4, 128], F32); r_msk = S.res()
        S.dma("sp", "vecmsk", msk[:], msk_d, writes=[r_msk])
        AqT = S.sb("AqT", [128, 8, 128], BF16); r_AqT = S.res()
        vd0 = S.sb("vd0", [128, 8, 128], BF16); r_vd0 = S.res()
        vnew = S.sb("vnew", [128, 8, 128], BF16); r_vnew = S.res()
        o1s = S.sb("o1s", [128, 8, 128], F32); r_o1s = S.res()
        ob = [S.sb("ob%d" % i, [128, 8, 128], F32) for i in range(2)]; r_ob = [S.res() for _ in range(2)]
        St = S.sb("St", [128, 8, 128], F32); r_St = S.res()
        Sb = S.sb("Sb", [128, 8, 128], BF16); r_Sb = S.res()
        S.op("pool", lambda e: e.memset(St[:], 0.0), writes=[r_St])
        S.op("pool", lambda e: e.memset(Sb[:], 0.0), writes=[r_Sb])
        S.op("pool", lambda e: e.memset(PC[0][:], 0.0), writes=[r_PC[0]])
        S.op("pool", lambda e: e.memset(PC[1][:], 0.0), writes=[r_PC[1]])

        def bc8(ap8):
            return ap8.unsqueeze(2).to_broadcast([128, 8, 128])

        def g4(t):
            return t.rearrange("p (k g) d -> p k g d", g=2)

        def kb(t4):
            return t4.unsqueeze(2).to_broadcast([128, 4, 2, 128])

        def project(c):
            seg = 1 if c < nch_ctx else 0
            b = c % 2
            S.dma("sp", "xt%d" % b, xt[b][:], xs[c * 128:(c + 1) * 128, :], writes=[r_xt[b]])
            fr.run(xt[b][:], r_xt[b], gsF[:, seg, :], vecF[:, 1 + 2 * seg, :], r_gsF, hT[:], r_hT)
            for half in range(2):
                Pt, r_Pt = getP()
                for mm in range(8):
                    m = half * 8 + mm
                    for kc in range(8):
                        S.op("pe", lambda e, Pt=Pt, mm=mm, m=m, kc=kc: e.matmul(Pt[:, mm, :], lhsT=winb[:, kc, m * 128:(m + 1) * 128], rhs=hT[:, kc, :], start=(kc == 0), stop=(kc == 7)),
                             reads=[r_winb, r_hT], writes=[r_Pt])
                S.op("act", lambda e, Pt=Pt, half=half, b=b: e.copy(out=PC[b][:, half * 8:(half + 1) * 8, 2:130], in_=Pt[:]), reads=[r_Pt], writes=[r_PC[b]])
            for kc in range(8):
                S.op("pe", lambda e, kc=kc: e.matmul(Bs[:, 0:16], lhsT=hT[:, kc, :], rhs=winb[:, kc, 2048:2064], start=(kc == 0), stop=(kc == 7)), reads=[r_winb, r_hT], writes=[r_Bs])
            S.op("act", lambda e, b=b: e.copy(out=abt[b][:], in_=Bs[:, 0:16]), reads=[r_Bs], writes=[r_abt[b]])
            same = (c >= 1) and ((c - 1 < nch_ctx) == (c < nch_ctx))
            if c >= 1:
                pb = (c - 1) % 2
                if same:
                    S.op("pool", lambda e, b=b, pb=pb: e.tensor_copy(out=PC[pb][:, :, 130:132], in_=PC[b][:, :, 2:4]), reads=[r_PC[b]], writes=[r_PC[pb]])
                    S.op("pool", lambda e, b=b, pb=pb: e.tensor_copy(out=PC[b][:, :, 0:2], in_=PC[pb][:, :, 128:130]), reads=[r_PC[pb]], writes=[r_PC[b]])
                else:
                    S.op("pool", lambda e, pb=pb: e.memset(PC[pb][:, :, 130:132], 0.0), writes=[r_PC[pb]])
                    S.op("pool", lambda e, b=b: e.memset(PC[b][:, :, 0:2], 0.0), writes=[r_PC[b]])

        def pre(c):
            b = c % 2
            if c == NCH - 1:
                S.op("pool", lambda e, b=b: e.memset(PC[b][:, :, 130:132], 0.0), writes=[r_PC[b]])
            for half in range(2):
                eng = "dve" if half == 0 else "pool"
                msl = slice(half * 8, (half + 1) * 8)
                for tap in range(5):
                    wv = cw[:, msl, tap].unsqueeze(2).to_broadcast([128, 8, 128])
                    src = PC[b][:, msl, tap:tap + 128]
                    if tap == 0:
                        S.op(eng, lambda e, src=src, wv=wv: e.tensor_tensor(out=cacc[:], in0=src, in1=wv, op=ALU.mult), reads=[r_PC[b], r_cw], writes=[r_cacc])
                    else:
                        S.op(eng, lambda e, src=src, wv=wv: e.tensor_tensor(out=ctmp[:], in0=src, in1=wv, op=ALU.mult), reads=[r_PC[b], r_cw], writes=[r_ctmp])
                        S.op(eng, lambda e: e.tensor_tensor(out=cacc[:], in0=cacc[:], in1=ctmp[:], op=ALU.add), reads=[r_cacc, r_ctmp], writes=[r_cacc])
                if half == 0:
                    S.op("act", lambda e: e.activation(out=Yqk[:], in_=cacc[:], func=AF.Silu), reads=[r_cacc], writes=[r_Yqk])
                else:
                    S.op("act", lambda e: e.activation(out=Vsl[:], in_=cacc[:], func=AF.Silu), reads=[r_cacc], writes=[r_Vsl])
            S.op("act", lambda e: e.activation(out=sqb[:], in_=Yqk[:], func=AF.Square), reads=[r_Yqk], writes=[r_sqb])
            Pq, r_Pq = getP()
            for hf in range(2):
                S.op("pe", lambda e, hf=hf, Pq=Pq: e.matmul(Pq[:, hf * 4:(hf + 1) * 4, :], lhsT=onesB[:], rhs=sqb[:, hf * 4:(hf + 1) * 4, :], start=True, stop=True), reads=[r_onesB, r_sqb], writes=[r_Pq])
            S.op("dve", lambda e, Pq=Pq: e.tensor_scalar(out=rn[:], in0=Pq[:], scalar1=EPS, scalar2=None, op0=ALU.add), reads=[r_Pq], writes=[r_rn])
            S.op("act", lambda e: e.activation(out=rn[:], in_=rn[:], func=AF.Sqrt), reads=[r_rn], writes=[r_rn])
            S.op("dve", lambda e: e.reciprocal(out=rn[:], in_=rn[:]), reads=[r_rn], writes=[r_rn])
            S.op("dve", lambda e: e.scalar_tensor_tensor(out=QT[:], in0=Yqk[:, 0:4, :], scalar=QS, in1=rn[:, 0:4, :], op0=ALU.mult, op1=ALU.mult), reads=[r_Yqk, r_rn], writes=[r_QT])
            S.op("pool", lambda e: e.tensor_tensor(out=KT[:], in0=Yqk[:, 4:8, :], in1=rn[:, 4:8, :], op=ALU.mult), reads=[r_Yqk, r_rn], writes=[r_KT])
            Pv, r_Pv = getP()
            Pvb = Pv[:].rearrange("p h d -> p (h d)").bitcast(BF16)
            for m in range(8):
                S.op("pe", lambda e, m=m, Pvb=Pvb: e.transpose(out=Pvb[:, m * 128:(m + 1) * 128], in_=Vsl[:, m, :], identity=ident[:]), reads=[r_Vsl, r_ident], writes=[r_Pv])
            for m in range(4):
                S.op("pe", lambda e, m=m, Pvb=Pvb: e.transpose(out=Pvb[:, 1024 + m * 128:1024 + (m + 1) * 128], in_=KT[:, m, :], identity=ident[:]), reads=[r_KT, r_ident], writes=[r_Pv])
            S.op("act", lambda e, Pvb=Pvb: e.copy(out=Vt[:].rearrange("p h d -> p (h d)"), in_=Pvb[:, 0:1024]), reads=[r_Pv], writes=[r_Vt])
            S.op("act", lambda e, Pvb=Pvb: e.copy(out=Ktok[:].rearrange("p h d -> p (h d)"), in_=Pvb[:, 1024:1536]), reads=[r_Pv], writes=[r_Ktok])
            a_ = abt[b][:, 0:8]
            b_ = abt[b][:, 8:16]
            G_, be_, Gc, eG, neG, eGl, eGlast, tm = [gt[:, i, :] for i in range(8)]
            S.op("dve", lambda e: e.tensor_tensor(out=tm, in0=a_, in1=al[:, 8:16], op=ALU.add), reads=[r_abt[b], r_al], writes=[r_gt])
            S.op("act", lambda e: e.activation(out=tm, in_=tm, func=AF.Exp), reads=[r_gt], writes=[r_gt])
            S.op("act", lambda e: e.activation(out=tm, in_=tm, func=AF.Ln, bias=1.0), reads=[r_gt], writes=[r_gt])
            S.op("dve", lambda e: e.tensor_tensor(out=G_, in0=tm, in1=al[:, 0:8], op=ALU.mult), reads=[r_gt, r_al], writes=[r_gt])
            S.op("act", lambda e: e.activation(out=be_, in_=b_, func=AF.Exp, scale=-1.0), reads=[r_abt[b]], writes=[r_gt])
            S.op("dve", lambda e: e.tensor_scalar(out=be_, in0=be_, scalar1=1.0, scalar2=None, op0=ALU.add), reads=[r_gt], writes=[r_gt])
            S.op("dve", lambda e: e.reciprocal(out=be_, in_=be_), reads=[r_gt], writes=[r_gt])
            S.op("pe", lambda e: e.matmul(Bs[:, 16:24], lhsT=U[:], rhs=G_, start=True, stop=True), reads=[r_U, r_gt], writes=[r_Bs])
            S.op("pe", lambda e: e.matmul(Bs[:, 24:32], lhsT=onesF[:], rhs=G_, start=True, stop=True), reads=[r_onesF, r_gt], writes=[r_Bs])
            S.op("act", lambda e: e.copy(out=Gc, in_=Bs[:, 16:24]), reads=[r_Bs], writes=[r_gt])
            S.op("act", lambda e: e.activation(out=eG, in_=Bs[:, 16:24], func=AF.Exp), reads=[r_Bs], writes=[r_gt])
            S.op("dve", lambda e: e.tensor_scalar(out=neG, in0=eG, scalar1=-1.0, scalar2=None, op0=ALU.mult), reads=[r_gt], writes=[r_gt])
            S.op("dve", lambda e: e.tensor_tensor(out=eGl, in0=Bs[:, 24:32], in1=Gc, op=ALU.subtract), reads=[r_Bs, r_gt], writes=[r_gt])
            S.op("act", lambda e: e.activation(out=eGl, in_=eGl, func=AF.Exp), reads=[r_gt], writes=[r_gt])
            S.op("act", lambda e: e.activation(out=eGlast, in_=Bs[:, 24:32], func=AF.Exp), reads=[r_Bs], writes=[r_gt])
            S.op("pool", lambda e: e.tensor_tensor(out=g4(kd[:]), in0=kb(Ktok[:]), in1=eGl.rearrange("p (k g) -> p k g", g=2).unsqueeze(3).to_broadcast([128, 4, 2, 128]), op=ALU.mult), reads=[r_Ktok, r_gt], writes=[r_kd])
            Pk, r_Pk = getP()
            for m in range(4):
                S.op("pe", lambda e, m=m, Pk=Pk: e.matmul(Pk[:, m, :], lhsT=KT[:, m, :], rhs=KT[:, m, :], start=True, stop=True), reads=[r_KT], writes=[r_Pk])
                S.op("pe", lambda e, m=m, Pk=Pk: e.matmul(Pk[:, 4 + m, :], lhsT=KT[:, m, :], rhs=QT[:, m, :], start=True, stop=True), reads=[r_KT, r_QT], writes=[r_Pk])
            S.op("act", lambda e, Pk=Pk: e.copy(out=kk[:], in_=Pk[:, 0:4, :]), reads=[r_Pk], writes=[r_kk])
            S.op("act", lambda e, Pk=Pk: e.copy(out=qk[:], in_=Pk[:, 4:8, :]), reads=[r_Pk], writes=[r_qk])
            S.op("pool", lambda e: e.tensor_tensor(out=dG[:], in0=identF[:].unsqueeze(1).to_broadcast([128, 8, 128]), in1=bc8(Gc), op=ALU.mult), reads=[r_identF, r_gt], writes=[r_dG])
            Pg, r_Pg = getP()
            for hf in range(2):
                S.op("pe", lambda e, hf=hf, Pg=Pg: e.matmul(Pg[:, hf * 4:(hf + 1) * 4, :], lhsT=onesF[:], rhs=dG[:, hf * 4:(hf + 1) * 4, :], start=True, stop=True), reads=[r_onesF, r_dG], writes=[r_Pg])
            S.op("dve", lambda e, Pg=Pg: e.tensor_tensor(out=DT[:], in0=Pg[:], in1=bc8(Gc), op=ALU.subtract), reads=[r_Pg, r_gt], writes=[r_DT])
            S.op("pool", lambda e: e.tensor_scalar(out=DT[:], in0=DT[:], scalar1=0.0, scalar2=None, op0=ALU.min), reads=[r_DT], writes=[r_DT])
            S.op("act", lambda e: e.activation(out=DT[:], in_=DT[:], func=AF.Exp), reads=[r_DT], writes=[r_DT])
            S.op("dve", lambda e: e.scalar_tensor_tensor(out=nbSU[:], in0=SU[:].unsqueeze(1).to_broadcast([128, 8, 128]), scalar=-1.0, in1=bc8(be_), op0=ALU.mult, op1=ALU.mult), reads=[r_SU, r_gt], writes=[r_nbSU])
            S.op("dve", lambda e: e.tensor_tensor(out=g4(t1[:]), in0=kb(kk[:]), in1=g4(DT[:]), op=ALU.mult), reads=[r_kk, r_DT], writes=[r_t1])
            S.op("dve", lambda e: e.tensor_tensor(out=X0t[:], in0=t1[:], in1=nbSU[:], op=ALU.mult), reads=[r_t1, r_nbSU], writes=[r_X0t])
            S.op("pool", lambda e: e.tensor_tensor(out=g4(t1[:]), in0=kb(qk[:]), in1=g4(DT[:]), op=ALU.mult), reads=[r_qk, r_DT], writes=[r_t1])
            S.op("pool", lambda e: e.tensor_tensor(out=AqT[:], in0=t1[:], in1=U[:].unsqueeze(1).to_broadcast([128, 8, 128]), op=ALU.mult), reads=[r_t1, r_U], writes=[r_AqT])
            Py, r_Py = getP()
            for h in range(8):
                S.op("pe", lambda e, h=h, Py=Py: e.transpose(out=Py[:, h, :], in_=X0t[:, h, :], identity=identF[:]), reads=[r_X0t, r_identF], writes=[r_Py])
            S.op("act", lambda e, Py=Py: e.copy(out=Y0t[:], in_=Py[:]), reads=[r_Py], writes=[r_Y0t])

            def mk(i):
                return msk[:, i, :].unsqueeze(1).to_broadcast([128, 8, 128])

            S.op("pool", lambda e: e.tensor_tensor(out=Xb[0][:], in0=X0t[:], in1=mk(0), op=ALU.mult), reads=[r_X0t, r_msk], writes=[r_Xb[0]])
            S.op("pool", lambda e: e.tensor_tensor(out=Yb[0][:], in0=Y0t[:], in1=mk(0), op=ALU.mult), reads=[r_Y0t, r_msk], writes=[r_Yb[0]])
            S.op("pool", lambda e: e.tensor_tensor(out=Qx[0][:], in0=Xb[0][:], in1=identF[:].unsqueeze(1).to_broadcast([128, 8, 128]), op=ALU.add), reads=[r_Xb[0], r_identF], writes=[r_Qx[0]])
            S.op("pool", lambda e: e.tensor_tensor(out=Qy[0][:], in0=Yb[0][:], in1=identF[:].unsqueeze(1).to_broadcast([128, 8, 128]), op=ALU.add), reads=[r_Yb[0], r_identF], writes=[r_Qy[0]])
            NL = 3
            for l in range(1, NL + 1):
                p_, n_ = (l - 1) % 2, l % 2
                Px, r_Px = getP()
                for h in range(8):
                    S.op("pe", lambda e, h=h, Px=Px, p_=p_: e.matmul(Px[:, h, :], lhsT=Yb[p_][:, h, :], rhs=Xb[p_][:, h, :], start=True, stop=True), reads=[r_Yb[p_], r_Xb[p_]], writes=[r_Px])
                Py, r_Py = getP()
                for h in range(8):
                    S.op("pe", lambda e, h=h, Py=Py, p_=p_: e.matmul(Py[:, h, :], lhsT=Xb[p_][:, h, :], rhs=Yb[p_][:, h, :], start=True, stop=True), reads=[r_Yb[p_], r_Xb[p_]], writes=[r_Py])
                S.op("act", lambda e, Px=Px, n_=n_: e.copy(out=Xb[n_][:], in_=Px[:]), reads=[r_Px], writes=[r_Xb[n_]])
                S.op("act", lambda e, Py=Py, n_=n_: e.copy(out=Yb[n_][:], in_=Py[:]), reads=[r_Py], writes=[r_Yb[n_]])
                Pqx, r_Pqx = getP()
                for h in range(8):
                    S.op("pe", lambda e, h=h, Pqx=Pqx, p_=p_, n_=n_: e.matmul(Pqx[:, h, :], lhsT=Qy[p_][:, h, :], rhs=Xb[n_][:, h, :], start=True, stop=True), reads=[r_Qy[p_], r_Xb[n_]], writes=[r_Pqx])
                Pqy, r_Pqy = getP()
                for h in range(8):
                    S.op("pe", lambda e, h=h, Pqy=Pqy, p_=p_, n_=n_: e.matmul(Pqy[:, h, :], lhsT=Qx[p_][:, h, :], rhs=Yb[n_][:, h, :], start=True, stop=True), reads=[r_Qx[p_], r_Yb[n_]], writes=[r_Pqy])
                S.op("dve", lambda e, Pqx=Pqx, p_=p_, n_=n_: e.tensor_tensor(out=Qx[n_][:], in0=Pqx[:], in1=Qx[p_][:], op=ALU.add), reads=[r_Pqx, r_Qx[p_]], writes=[r_Qx[n_]])
                S.op("dve", lambda e, Pqy=Pqy, p_=p_, n_=n_: e.tensor_tensor(out=Qy[n_][:], in0=Pqy[:], in1=Qy[p_][:], op=ALU.add), reads=[r_Pqy, r_Qy[p_]], writes=[r_Qy[n_]])
            cur = NL % 2
            for mi in (1, 2, 3):
                lastm = (mi == 3)
                nxt = 1 - cur
                S.op("pool", lambda e, mi=mi: e.tensor_tensor(out=Xb[0][:], in0=X0t[:], in1=mk(mi), op=ALU.mult), reads=[r_X0t, r_msk], writes=[r_Xb[0]])
                S.op("pool", lambda e, mi=mi: e.tensor_tensor(out=Yb[0][:], in0=Y0t[:], in1=mk(mi), op=ALU.mult), reads=[r_Y0t, r_msk], writes=[r_Yb[0]])
                Pw, r_Pw = getP()
                for h in range(8):
                    S.op("pe", lambda e, h=h, Pw=Pw, cur=cur: e.matmul(Pw[:, h, :], lhsT=Yb[0][:, h, :], rhs=Qx[cur][:, h, :], start=True, stop=True), reads=[r_Yb[0], r_Qx[cur]], writes=[r_Pw])
                S.op("act", lambda e, Pw=Pw: e.copy(out=Xb[1][:], in_=Pw[:]), reads=[r_Pw], writes=[r_Xb[1]])
                Pd, r_Pd = getP()
                for h in range(8):
                    S.op("pe", lambda e, h=h, Pd=Pd, cur=cur: e.matmul(Pd[:, h, :], lhsT=Qy[cur][:, h, :], rhs=Xb[1][:, h, :], start=True, stop=True), reads=[r_Qy[cur], r_Xb[1]], writes=[r_Pd])
                if not lastm:
                    Pw2, r_Pw2 = getP()
                    for h in range(8):
                        S.op("pe", lambda e, h=h, Pw2=Pw2, cur=cur: e.matmul(Pw2[:, h, :], lhsT=Xb[0][:, h, :], rhs=Qy[cur][:, h, :], start=True, stop=True), reads=[r_Xb[0], r_Qy[cur]], writes=[r_Pw2])
                    S.op("act", lambda e, Pw2=Pw2: e.copy(out=Yb[1][:], in_=Pw2[:]), reads=[r_Pw2], writes=[r_Yb[1]])
                    Pd2, r_Pd2 = getP()
                    for h in range(8):
                        S.op("pe", lambda e, h=h, Pd2=Pd2, cur=cur: e.matmul(Pd2[:, h, :], lhsT=Qx[cur][:, h, :], rhs=Yb[1][:, h, :], start=True, stop=True), reads=[r_Qx[cur], r_Yb[1]], writes=[r_Pd2])
                    S.op("dve", lambda e, Pd=Pd, cur=cur, nxt=nxt: e.tensor_tensor(out=Qx[nxt][:], in0=Pd[:], in1=Qx[cur][:], op=ALU.add), reads=[r_Pd, r_Qx[cur]], writes=[r_Qx[nxt]])
                    S.op("dve", lambda e, Pd2=Pd2, cur=cur, nxt=nxt: e.tensor_tensor(out=Qy[nxt][:], in0=Pd2[:], in1=Qy[cur][:], op=ALU.add), reads=[r_Pd2, r_Qy[cur]], writes=[r_Qy[nxt]])
                    cur = nxt
                else:
                    S.op("dve", lambda e, Pd=Pd, cur=cur: e.tensor_tensor(out=TT[:], in0=Pd[:], in1=Qx[cur][:], op=ALU.add), reads=[r_Pd, r_Qx[cur]], writes=[r_TT])

        def seq(c):
            b = c % 2
            G_, be_, Gc, eG, neG, eGl, eGlast, tm = [gt[:, i, :] for i in range(8)]
            Pa, r_Pa = getP()
            for h in range(8):
                S.op("pe", lambda e, h=h, Pa=Pa: e.matmul(Pa[:, h, :], lhsT=KT[:, h // 2, :], rhs=Sb[:, h, :], start=True, stop=True), reads=[r_KT, r_Sb], writes=[r_Pa])
            S.op("dve", lambda e, Pa=Pa: e.tensor_tensor(out=o1s[:], in0=Pa[:], in1=bc8(neG), op=ALU.mult), reads=[r_Pa, r_gt], writes=[r_o1s])
            S.op("pool", lambda e: e.tensor_tensor(out=vd0[:], in0=o1s[:], in1=Vt[:], op=ALU.add), reads=[r_o1s, r_Vt], writes=[r_vd0])
            Pn, r_Pn = getP()
            for h in range(8):
                S.op("pe", lambda e, h=h, Pn=Pn: e.matmul(Pn[:, h, :], lhsT=TT[:, h, :], rhs=vd0[:, h, :], start=True, stop=True), reads=[r_TT, r_vd0], writes=[r_Pn])
            S.op("dve", lambda e, Pn=Pn: e.tensor_tensor(out=vnew[:], in0=Pn[:], in1=bc8(be_), op=ALU.mult), reads=[r_Pn, r_gt], writes=[r_vnew])
            Po1, r_Po1 = getP()
            for h in range(8):
                S.op("pe", lambda e, h=h, Po1=Po1: e.matmul(Po1[:, h, :], lhsT=QT[:, h // 2, :], rhs=Sb[:, h, :], start=True, stop=True), reads=[r_QT, r_Sb], writes=[r_Po1])
            Po2, r_Po2 = getP()
            for h in range(8):
                S.op("pe", lambda e, h=h, Po2=Po2: e.matmul(Po2[:, h, :], lhsT=AqT[:, h, :], rhs=vnew[:, h, :], start=True, stop=True), reads=[r_AqT, r_vnew], writes=[r_Po2])
            S.op("dve", lambda e, Po1=Po1: e.tensor_tensor(out=o1s[:], in0=Po1[:], in1=bc8(eG), op=ALU.mult), reads=[r_Po1, r_gt], writes=[r_o1s])
            S.op("dve", lambda e, Po2=Po2, b=b: e.tensor_tensor(out=ob[b][:], in0=Po2[:], in1=o1s[:], op=ALU.add), reads=[r_Po2, r_o1s], writes=[r_ob[b]])
            Ps, r_Ps = getP()
            for h in range(8):
                S.op("pe", lambda e, h=h, Ps=Ps: e.matmul(Ps[:, h, :], lhsT=kd[:, h, :], rhs=vnew[:, h, :], start=True, stop=True), reads=[r_kd, r_vnew], writes=[r_Ps])
            S.op("pool", lambda e: e.tensor_tensor(out=St[:], in0=St[:], in1=bc8(eGlast), op=ALU.mult), reads=[r_St, r_gt], writes=[r_St])
            S.op("dve", lambda e, Ps=Ps: e.tensor_tensor(out=St[:], in0=St[:], in1=Ps[:], op=ALU.add), reads=[r_St, r_Ps], writes=[r_St])
            S.op("act", lambda e: e.copy(out=Sb[:], in_=St[:]), reads=[r_St], writes=[r_Sb])
            r_o = S.res()
            S.dma("sp", "out%d" % b, out[c * 128:(c + 1) * 128, :], ob[b][:].rearrange("p h d -> p (h d)"), reads=[r_ob[b]], writes=[r_o])
            return r_o

        last = []
        S.barrier()
        project(0)
        for c in range(NCH):
            if c + 1 < NCH:
                project(c + 1)
            pre(c)
            last.append(seq(c))
        S.final_wait("sp", last[-2:])
        S.emit()
    return nc


def _launch(nc, in_maps):
    res = run_bass_kernel_spmd(nc, in_maps, core_ids=list(range(NCORES)))
    return [r["out"] for r in res.results]


def _vecF(rows):
    v = np.stack(rows, 0).astype(np.float32)
    return np.ascontiguousarray(v.reshape(len(rows), 8, 128).transpose(2, 0, 1))


def _rope_table(n_tok):
    t = np.arange(n_tok)
    row = (t // 64).astype(np.float32)
    col = (t % 64).astype(np.float32)
    inv = np.power(np.float32(10000.0), -np.arange(0, 64, 2, dtype=np.float32) / np.float32(64)).astype(np.float32)
    ar = (row[:, None] * inv).astype(np.float32)
    ac = (col[:, None] * inv).astype(np.float32)
    return np.concatenate([np.cos(ar), np.sin(ar), np.cos(ac), np.sin(ac)], 1).astype(np.float32)


def _dn_masks():
    i = np.arange(128)
    def bd(sz):
        return ((i[:, None] // sz) == (i[None, :] // sz)).astype(np.float32)
    m = np.stack([bd(16), bd(32) - bd(16), bd(64) - bd(32), 1.0 - bd(64)], 0)
    return np.ascontiguousarray(m.transpose(1, 0, 2))


def _rows_tok(xa, ca, i):
    b, j = i // 4, i % 4
    pad = np.zeros((64,) + xa.shape[2:], np.float32)
    cflat = ca.reshape((512,) + ca.shape[2:])
    return np.ascontiguousarray(np.concatenate([xa[b, j * 4096:(j + 1) * 4096], cflat[i * 64:(i + 1) * 64], pad], 0))


def _gather_tok(outs):
    x = np.stack([np.concatenate([outs[b * 4 + j][:4096] for j in range(4)], 0) for b in range(2)], 0)
    c = np.concatenate([outs[i][4096:4160] for i in range(8)], 0).reshape(2, 256, -1)
    return x, c


_PROG = {}


def _prog(key, fn, *a):
    if key not in _PROG:
        _PROG[key] = fn(*a)
    return _PROG[key]


def kernel(x, c, ctx, c_ctx, ada_w, ada_b, norm_mix_g, norm_ffn_g, ffn_w1, ffn_w2,
           dn_w_in, dn_conv_w, dn_a_log, dn_dt_bias, dn_norm_g, dn_w_out,
           gla_w_in, gla_gate_w2, gla_gate_b2, gla_norm_g, gla_w_out,
           attn_w_in, attn_q_norm_g, attn_k_norm_g, attn_w_out):
    f = lambda a: np.asarray(a, dtype=np.float32)
    x, c, ctx, c_ctx, ada_w, ada_b = f(x), f(c), f(ctx), f(c_ctx), f(ada_w), f(ada_b)
    norm_mix_g, norm_ffn_g, ffn_w1, ffn_w2 = f(norm_mix_g), f(norm_ffn_g), f(ffn_w1), f(ffn_w2)
    dn_w_in, dn_conv_w, dn_a_log, dn_dt_bias, dn_norm_g, dn_w_out = f(dn_w_in), f(dn_conv_w), f(dn_a_log), f(dn_dt_bias), f(dn_norm_g), f(dn_w_out)
    gla_w_in, gla_gate_w2, gla_gate_b2, gla_norm_g, gla_w_out = f(gla_w_in), f(gla_gate_w2), f(gla_gate_b2), f(gla_norm_g), f(gla_w_out)
    attn_w_in, attn_q_norm_g, attn_k_norm_g, attn_w_out = f(attn_w_in), f(attn_q_norm_g), f(attn_k_norm_g), f(attn_w_out)

    mod = run_mod(c, c_ctx, ada_w, ada_b)
    DEPTH = 4
    for i in range(DEPTH):
        mix, slot = i % 3, i // 3
        m = [mod[i, :, j * 1024:(j + 1) * 1024] for j in range(6)]
        vF_mix = [_vecF([norm_mix_g[i], m[0][b], m[1][b], m[0][2], m[1][2]]) for b in range(2)]
        if mix == 0:
            nc = _prog("dn", build_dn)
            w = dn_w_in[slot]
            in_maps = []
            for cid in range(8):
                b, d, hf = cid // 4, (cid // 2) % 2, cid % 2
                seq = np.concatenate([ctx[b][::-1], x[b][::-1]] if d else [ctx[b], x[b]], 0)
                qc = np.arange(hf * 512, (hf + 1) * 512)
                kc_ = 1024 + qc
                vc = 2048 + np.arange(hf * 1024, (hf + 1) * 1024)
                ac = 6144 + d * 16 + hf * 8 + np.arange(8)
                bc_ = 6144 + 32 + d * 16 + hf * 8 + np.arange(8)
                cols = np.concatenate([qc, kc_, vc, ac, bc_])
                cwf = dn_conv_w[slot][np.concatenate([qc, kc_, vc])]
                if d:
                    cwf = cwf[:, ::-1]
                in_maps.append({
                    "xs": np.ascontiguousarray(seq), "vecF": vF_mix[b], "win": np.ascontiguousarray(w[:, cols]),
                    "cw": np.ascontiguousarray(cwf.reshape(16, 128, 5).transpose(1, 0, 2)),
                    "al": np.concatenate([dn_a_log[slot][d, hf * 8:(hf + 1) * 8], dn_dt_bias[slot][d, hf * 8:(hf + 1) * 8]])[None].astype(np.float32),
                    "msk": _dn_masks()})
            outs = _launch(nc, in_maps)
            VD, HD, NDIR, gated = 2048, 128, 2, True
            Ox = np.zeros((2, 16384, 2, VD), np.float32)
            Oc = np.zeros((2, 256, 2, VD), np.float32)
            for cid in range(8):
                b, d, hf = cid // 4, (cid // 2) % 2, cid % 2
                oc_, ox_ = outs[cid][:256], outs[cid][256:]
                if d:
                    oc_, ox_ = oc_[::-1], ox_[::-1]
                Ox[b, :, d, hf * 1024:(hf + 1) * 1024] = ox_
                Oc[b, :, d, hf * 1024:(hf + 1) * 1024] = oc_
            ng = np.tile(dn_norm_g[slot], 16)[None]
            wz, wout = w[:, 4096:6144], dn_w_out[slot]
        elif mix == 1:
            nc = _prog("gla", build_gla)
            w = gla_w_in[slot]
            in_maps = []
            for cid in range(8):
                b, d, hp = cid // 4, (cid // 2) % 2, cid % 2
                seq = np.concatenate([ctx[b][::-1], x[b][::-1]] if d else [ctx[b], x[b]], 0)
                qc = np.arange(hp * 256, (hp + 1) * 256)
                cols = np.concatenate([qc, 512 + qc, 1024 + np.arange(hp * 512, (hp + 1) * 512), 3072 + d * 16 + np.arange(16)])
                in_maps.append({
                    "xs": np.ascontiguousarray(seq), "vecF": vF_mix[b], "win": np.ascontiguousarray(w[:, cols]),
                    "gw2": np.ascontiguousarray(gla_gate_w2[slot][d][:, qc]), "gb2": np.ascontiguousarray(gla_gate_b2[slot][d][qc][None])})
            outs = _launch(nc, in_maps)
            VD, HD, NDIR, gated = 1024, 256, 2, True
            Ox = np.zeros((2, 16384, 2, VD), np.float32)
            Oc = np.zeros((2, 256, 2, VD), np.float32)
            for cid in range(8):
                b, d, hp = cid // 4, (cid // 2) % 2, cid % 2
                oc_, ox_ = outs[cid][:256], outs[cid][256:]
                if d:
                    oc_, ox_ = oc_[::-1], ox_[::-1]
                Ox[b, :, d, hp * 512:(hp + 1) * 512] = ox_
                Oc[b, :, d, hp * 512:(hp + 1) * 512] = oc_
            ng = np.tile(gla_norm_g[slot], 4)[None]
            wz, wout = w[:, 2048:3072], gla_w_out[slot]
        else:
            nc = _prog("attn", build_attn)
            w = attn_w_in[slot]
            tab = np.concatenate([_rope_table(16384), np.tile(np.concatenate([np.ones(32), np.zeros(32)] * 2)[None], (256, 1))], 0).astype(np.float32)
            gn = np.concatenate([attn_q_norm_g[slot], attn_q_norm_g[slot], attn_k_norm_g[slot]])[None]
            in_maps = []
            for cid in range(8):
                b, j = cid // 4, cid % 4
                kv = j // 2
                cols = np.concatenate([np.arange(2 * j * 128, (2 * j + 2) * 128), 1024 + kv * 128 + np.arange(128), 1280 + kv * 128 + np.arange(128)])
                in_maps.append({"xs": np.ascontiguousarray(np.concatenate([x[b], ctx[b]], 0)), "vecF": vF_mix[b],
                                "win": np.ascontiguousarray(w[:, cols]), "gn": gn, "tab": tab})
            outs = _launch(nc, in_maps)
            VD, HD, NDIR, gated = 1024, 128, 1, False
            Ox = np.zeros((2, 16384, 1, VD), np.float32)
            Oc = np.zeros((2, 256, 1, VD), np.float32)
            for cid in range(8):
                b, j = cid // 4, cid % 4
                Ox[b, :, 0, 2 * j * 128:(2 * j + 2) * 128] = outs[cid][:16384]
                Oc[b, :, 0, 2 * j * 128:(2 * j + 2) * 128] = outs[cid][16384:]
            ng = np.ones((1, VD), np.float32)
            wz, wout = np.zeros((1024, VD), np.float32), attn_w_out[slot]
        nc = _prog(("posta", VD, HD, gated, NDIR), build_posta, VD, HD, gated, NDIR)
        in_maps = []
        for cid in range(8):
            b = cid // 4
            in_maps.append({"x": _rows_tok(x, ctx, cid), "o": _rows_tok(Ox, Oc, cid), "vecF": vF_mix[b],
                            "vecT": np.ascontiguousarray(np.stack([m[2][b], m[2][2]], 0)), "ng": np.ascontiguousarray(ng.astype(np.float32)),
                            "wz": np.ascontiguousarray(wz), "wout": np.ascontiguousarray(wout)})
        x, ctx = _gather_tok(_launch(nc, in_maps))
        nc = _prog("ffn", build_ffn)
        in_maps = []
        for cid in range(8):
            b = cid // 4
            in_maps.append({"x": _rows_tok(x, ctx, cid), "vecF": _vecF([norm_ffn_g[i], m[3][b], m[4][b], m[3][2], m[4][2]]),
                            "vecT": np.ascontiguousarray(np.stack([m[5][b], m[5][2]], 0)), "w1": ffn_w1[i], "w2": ffn_w2[i]})
        x, ctx = _gather_tok(_launch(nc, in_maps))
    return np.ascontiguousarray(x.astype(np.float32))
```
